# Optimizing a Trainium2 kernel written in Bass

```python
import jax, jax.numpy as jnp
from jax import lax
import numpy as np

D_MODEL = 1024
BATCH = 4
SEQ = 4096
DEPTH = 2

CHUNK = 64
N_MIXERS = 2
N_RWKV = (DEPTH + 1) // 2
N_MLSTM = DEPTH // 2
EPS = 1e-6
NEG = -1e30

RW_HEAD = 64
RW_HEADS = D_MODEL // RW_HEAD
LORA_W = 64
LORA_A = 64
LORA_G = 160
RW_SEGS = (D_MODEL, LORA_W, D_MODEL, D_MODEL, LORA_A, LORA_G)
RW_IN = sum(RW_SEGS)
GN_EPS = 64e-5

ML_HEADS = 8
ML_QK = D_MODEL // 2 // ML_HEADS
ML_V = D_MODEL // ML_HEADS
ML_QKW = 2 * ML_HEADS * ML_QK
ML_VW = ML_HEADS * ML_V
ML_IN = ML_QKW + 2 * ML_VW + 2 * ML_HEADS
CONV_W = 4
GATE_CAP = 15.0

D_FF = -(-(8 * D_MODEL) // (3 * 256)) * 256

kernel_name = "rwkv7_mlstm_interleaved_adaln_trunk"


def _rms(x):
    xf = x.astype(jnp.float32)
    return (xf * lax.rsqrt(jnp.mean(xf * xf, -1, keepdims=True) + EPS)).astype(x.dtype)


def _softcap(t):
    return GATE_CAP * jnp.tanh(t / GATE_CAP)


def _token_shift(x):
    return jnp.pad(x[:, :-1], ((0, 0), (1, 0), (0, 0)))


def _rwkv7_step(state, inp):
    r_t, w_t, k_t, v_t, a_t, b_t = inp
    sa = jnp.einsum('bhvk,bhk->bhv', state, a_t)
    state = state * w_t[:, :, None, :] + sa[..., None] * b_t[:, :, None, :] + v_t[..., None] * k_t[:, :, None, :]
    y = jnp.einsum('bhvk,bhk->bhv', state, r_t)
    return state, y


def _rwkv7_mix(x, mix, w_in, w0, w2, a0, a2, g2, k_k, k_a, r_k, gn_w, gn_b, w_out):
    B, S, D = x.shape
    xx = _token_shift(x) - x
    xs = x[None] + xx[None] * mix[:, None, None, :]
    outs = []
    off = 0
    for i, n in enumerate(RW_SEGS):
        outs.append(xs[i] @ w_in[:, off:off + n])
        off += n
    r, wl, k, v, al, gl = outs
    w_log = -jax.nn.softplus(-(w0 + jnp.tanh(wl) @ w2)) - 0.5
    decay = jnp.exp(-jnp.exp(w_log.astype(jnp.float32)))
    a = jax.nn.sigmoid(a0 + al @ a2)
    g = jax.nn.sigmoid(gl) @ g2
    hs = lambda t: t.reshape(B, S, RW_HEADS, RW_HEAD).astype(jnp.float32)
    kk = hs(k * k_k)
    kk = kk / jnp.maximum(jnp.linalg.norm(kk, axis=-1, keepdims=True), 1e-12)
    k = k * (1.0 + (a - 1.0) * k_a)
    rh, kh, vh, ah, wh = hs(r), hs(k), hs(v), hs(a), hs(decay)
    tm = lambda t: jnp.moveaxis(t, 1, 0)
    state0 = jnp.zeros((B, RW_HEADS, RW_HEAD, RW_HEAD), jnp.float32)
    _, y = lax.scan(_rwkv7_step, state0, (tm(rh), tm(wh), tm(kh), tm(vh), tm(-kk), tm(kk * ah)))
    y = jnp.moveaxis(y, 0, 1)
    mu = jnp.mean(y, -1, keepdims=True)
    var = jnp.mean(jnp.square(y - mu), -1, keepdims=True)
    y = ((y - mu) * lax.rsqrt(var + GN_EPS)).reshape(B, S, D) * gn_w + gn_b
    bonus = (jnp.sum(rh * kh * r_k, -1, keepdims=True) * vh).reshape(B, S, D)
    return ((y + bonus).astype(x.dtype) * g) @ w_out


def _causal_dwconv(x, w, b):
    y = lax.conv_general_dilated(x, w[:, None, :], window_strides=(1,), padding=[(CONV_W - 1, 0)],
                                 dimension_numbers=('NWC', 'WIO', 'NWC'), feature_group_count=x.shape[-1])
    return y + b


def _mlstm_chunk_step(carry, inp):
    C, n, m = carry
    g_c, m_l, C_l, n_l = inp
    m_new = jnp.maximum(g_c + m, m_l)
    sa = jnp.exp(g_c + m - m_new)
    sb = jnp.exp(m_l - m_new)
    C_new = sa[..., None, None] * C + sb[..., None, None] * C_l
    n_new = sa[..., None] * n + sb[..., None] * n_l
    return (C_new, n_new, m_new), (C, n, m)


def _mlstm_mix(x, w_in, conv_w, conv_b, b_i, b_f, hn_w, w_out):
    B, S, D = x.shape
    NC = S // CHUNK
    p = x @ w_in
    qk = jax.nn.silu(_causal_dwconv(p[..., :ML_QKW], conv_w, conv_b))
    v = p[..., ML_QKW:ML_QKW + ML_VW]
    o = p[..., ML_QKW + ML_VW:ML_QKW + 2 * ML_VW]
    gi = p[..., ML_QKW + 2 * ML_VW:ML_QKW + 2 * ML_VW + ML_HEADS]
    gf = p[..., ML_QKW + 2 * ML_VW + ML_HEADS:]
    chunks = lambda t, dh: t.reshape(B, NC, CHUNK, ML_HEADS, dh).transpose(0, 3, 1, 2, 4).astype(jnp.float32)
    q = chunks(qk[..., :ML_QKW // 2], ML_QK) * (ML_QK ** -0.5)
    k = chunks(qk[..., ML_QKW // 2:], ML_QK)
    v = chunks(v, ML_V)
    gate_chunks = lambda t: t.reshape(B, NC, CHUNK, ML_HEADS).transpose(0, 3, 1, 2)
    log_i = gate_chunks(_softcap(gi.astype(jnp.float32) + b_i))
    log_f = gate_chunks(jax.nn.log_sigmoid(_softcap(gf.astype(jnp.float32) + b_f)))
    bcum = jnp.cumsum(log_f, -1)
    causal = jnp.tril(jnp.ones((CHUNK, CHUNK), dtype=bool))
    d_log = jnp.where(causal, bcum[..., :, None] - bcum[..., None, :] + log_i[..., None, :], NEG)
    m_intra = jnp.max(d_log, -1)
    g_tot = bcum[..., -1]
    loc = g_tot[..., None] - bcum + log_i
    m_loc = jnp.max(loc, -1)
    wl = jnp.exp(loc - m_loc[..., None])
    C_loc = jnp.einsum('bhcs,bhcsd,bhcse->bhcde', wl, k, v)
    n_loc = jnp.einsum('bhcs,bhcsd->bhcd', wl, k)
    init = (jnp.zeros((B, ML_HEADS, ML_QK, ML_V), jnp.float32),
            jnp.zeros((B, ML_HEADS, ML_QK), jnp.float32),
            jnp.zeros((B, ML_HEADS), jnp.float32))
    cm = lambda t: jnp.moveaxis(t, 2, 0)
    _, (C_prev, n_prev, m_prev) = lax.scan(_mlstm_chunk_step, init, (cm(g_tot), cm(m_loc), cm(C_loc), cm(n_loc)))
    C_prev, n_prev, m_prev = jnp.moveaxis(C_prev, 0, 2), jnp.moveaxis(n_prev, 0, 2), jnp.moveaxis(m_prev, 0, 2)
    inter = bcum + m_prev[..., None]
    m_t = jnp.maximum(inter, m_intra)
    w_inter = jnp.exp(inter - m_t)
    w_intra = jnp.exp(d_log - m_t[..., None]) * jnp.einsum('bhctd,bhcsd->bhcts', q, k)
    num = w_inter[..., None] * jnp.einsum('bhctd,bhcde->bhcte', q, C_prev) + jnp.einsum('bhcts,bhcse->bhcte', w_intra, v)
    den = w_inter * jnp.einsum('bhctd,bhcd->bhct', q, n_prev) + jnp.sum(w_intra, -1)
    den = jnp.maximum(jnp.abs(den), jnp.exp(-m_t))
    h = num / den[..., None]
    h = h * lax.rsqrt(jnp.mean(h * h, -1, keepdims=True) + EPS) * hn_w[:, None, None, :]
    h = h.transpose(0, 2, 3, 1, 4).reshape(B, S, ML_VW).astype(x.dtype)
    return (h * jax.nn.sigmoid(o)) @ w_out


def _swiglu(x, w_gu, w_down):
    gu = x @ w_gu
    return (jax.nn.silu(gu[..., :D_FF]) * gu[..., D_FF:]) @ w_down


def setup_inputs(seed: int = 0) -> dict:
    key = jax.random.key(seed)
    ks = jax.random.split(key, 32)
    nrm = lambda i, shape, s: jax.random.normal(ks[i], shape, jnp.float32) * s
    D = D_MODEL
    lin = jnp.arange(D, dtype=jnp.float32) / (D - 1)
    return {
        "x": nrm(0, (BATCH, SEQ, D), 1.0),
        "c": nrm(1, (BATCH, D), 1.0),
        "ada_w": nrm(2, (DEPTH, D, 6 * D), 0.3 * D ** -0.5),
        "ada_b": nrm(3, (DEPTH, 6 * D), 0.02),
        "rw_mix": jax.random.uniform(ks[4], (N_RWKV, 6, D), jnp.float32),
        "rw_w_in": nrm(5, (N_RWKV, D, RW_IN), D ** -0.5),
        "rw_w0": jnp.broadcast_to(-6.0 + 5.0 * lin ** 1.5, (N_RWKV, D)) + nrm(6, (N_RWKV, D), 0.1),
        "rw_w2": nrm(7, (N_RWKV, LORA_W, D), 0.3 * LORA_W ** -0.5),
        "rw_a0": nrm(8, (N_RWKV, D), 0.1),
        "rw_a2": nrm(9, (N_RWKV, LORA_A, D), LORA_A ** -0.5),
        "rw_g2": nrm(10, (N_RWKV, LORA_G, D), LORA_G ** -0.5),
        "rw_k_k": 0.85 + nrm(11, (N_RWKV, D), 0.05),
        "rw_k_a": 1.0 + nrm(12, (N_RWKV, D), 0.05),
        "rw_r_k": nrm(13, (N_RWKV, RW_HEADS, RW_HEAD), 0.1),
        "rw_gn_w": 1.0 + nrm(14, (N_RWKV, D), 0.05),
        "rw_gn_b": nrm(15, (N_RWKV, D), 0.02),
        "rw_w_out": nrm(16, (N_RWKV, D, D), D ** -0.5),
        "ml_w_in": nrm(17, (N_MLSTM, D, ML_IN), D ** -0.5),
        "ml_conv_w": nrm(18, (N_MLSTM, CONV_W, ML_QKW), CONV_W ** -0.5),
        "ml_conv_b": nrm(19, (N_MLSTM, ML_QKW), 0.02),
        "ml_b_i": nrm(20, (N_MLSTM, ML_HEADS), 0.1),
        "ml_b_f": jnp.broadcast_to(jnp.linspace(3.0, 6.0, ML_HEADS, dtype=jnp.float32), (N_MLSTM, ML_HEADS)) + nrm(21, (N_MLSTM, ML_HEADS), 0.1),
        "ml_hn_w": 1.0 + nrm(22, (N_MLSTM, ML_HEADS, ML_V), 0.05),
        "ml_w_out": nrm(23, (N_MLSTM, D, D), D ** -0.5),
        "ffn_w_gu": nrm(24, (DEPTH, D, 2 * D_FF), D ** -0.5),
        "ffn_w_down": nrm(25, (DEPTH, D_FF, D), D_FF ** -0.5),
        "final_w": 1.0 + nrm(26, (D,), 0.05),
    }


def reference(x, c, ada_w, ada_b, rw_mix, rw_w_in, rw_w0, rw_w2, rw_a0, rw_a2, rw_g2, rw_k_k, rw_k_a,
              rw_r_k, rw_gn_w, rw_gn_b, rw_w_out, ml_w_in, ml_conv_w, ml_conv_b, ml_b_i, ml_b_f, ml_hn_w,
              ml_w_out, ffn_w_gu, ffn_w_down, final_w):
    c_act = jax.nn.silu(c)
    for i in range(DEPTH):
        mod = c_act @ ada_w[i] + ada_b[i]
        sh1, sc1, g1, sh2, sc2, g2 = [t[:, None, :] for t in jnp.split(mod, 6, axis=-1)]
        h = _rms(x) * (1.0 + sc1) + sh1
        j = i // N_MIXERS
        if i % N_MIXERS == 0:
            y = _rwkv7_mix(h, rw_mix[j], rw_w_in[j], rw_w0[j], rw_w2[j], rw_a0[j], rw_a2[j], rw_g2[j],
                           rw_k_k[j], rw_k_a[j], rw_r_k[j], rw_gn_w[j], rw_gn_b[j], rw_w_out[j])
        else:
            y = _mlstm_mix(h, ml_w_in[j], ml_conv_w[j], ml_conv_b[j], ml_b_i[j], ml_b_f[j], ml_hn_w[j], ml_w_out[j])
        x = x + g1 * y
        h = _rms(x) * (1.0 + sc2) + sh2
        x = x + g2 * _swiglu(h, ffn_w_gu[i], ffn_w_down[i])
    return _rms(x) * final_w
```

```python
import contextlib
import os
import numpy as np
import concourse.bass as bass
import concourse.mybir as mybir
from concourse.bass_utils import run_bass_kernel_spmd

F32 = mybir.dt.float32
F32R = mybir.dt.float32r
BF16 = mybir.dt.bfloat16
AF = mybir.ActivationFunctionType
ALU = mybir.AluOpType

P = 128
D = 1024
KC = 8
S = 4096
DFF = 2816
RW_IN = 3360
ML_IN = 3088
EPS = 1e-6
GN_EPS = 64e-5
L = 128


class Buf:
    __slots__ = ("name", "ap", "w", "r")

    def __init__(self, name, ap):
        self.name = name
        self.ap = ap
        self.w = None
        self.r = {}

    def __getitem__(self, idx):
        return self.ap[idx]


class KB:
    def __init__(self, nc, stack, ndma=24):
        self.nc = nc
        self.stack = stack
        self.eng = {"pe": nc.tensor, "act": nc.scalar, "dve": nc.vector,
                    "pool": nc.gpsimd, "sp": nc.sync}
        self.sem = {}
        self.cnt = {}
        self.known = {e: {} for e in self.eng}
        for e in ["pe", "act", "dve", "pool"]:
            self.sem[e] = stack.enter_context(nc.semaphore("s_" + e))
            self.cnt[e] = 0
        self.ndma = ndma
        for i in range(ndma):
            self.sem[("d", i)] = stack.enter_context(nc.semaphore("d%d" % i))
            self.cnt[("d", i)] = 0
        self.dnext = 0
        self.hist = {}
        self.ninst = 0
        self.nwait = 0
        self.rr = 0

    def tile(self, name, shape, dtype=F32):
        return self.stack.enter_context(self.nc.sbuf_tensor("sb_" + name, list(shape), dtype))

    def sb(self, name, shape, dtype=F32):
        t = self.tile(name, shape, dtype)
        return Buf(name, t[:])

    def ps(self, name, shape, dtype=F32):
        t = self.stack.enter_context(self.nc.psum_tensor("ps_" + name, list(shape), dtype))
        return Buf(name, t[:])

    def _wait(self, e, deps):
        need = {}
        for k, v in deps:
            if v > need.get(k, 0):
                need[k] = v
        kn = self.known[e]
        for k, v in need.items():
            if e == "pe" and k == "pe":
                continue
            if kn.get(k, 0) >= v:
                continue
            self.eng[e].wait_ge(self.sem[k], v)
            kn[k] = v
            self.nwait += 1
            snap = self.hist.get((k, v))
            if snap:
                for k2, v2 in snap.items():
                    if kn.get(k2, 0) < v2:
                        kn[k2] = v2

    def _deps(self, reads, writes):
        deps = []
        for b in reads:
            if b.w:
                deps.append(b.w)
        for b in writes:
            if b.w:
                deps.append(b.w)
            deps.extend(b.r.items())
        return deps

    def _mark(self, tok, reads, writes):
        k, v = tok
        for b in reads:
            if b.r.get(k, 0) < v:
                b.r[k] = v
        for b in writes:
            b.w = tok
            b.r = {}

    def op(self, e, fn, reads=(), writes=()):
        self._wait(e, self._deps(reads, writes))
        inst = fn(self.eng[e])
        self.cnt[e] += 1
        inst.then_inc(self.sem[e], 1)
        tok = (e, self.cnt[e])
        self.hist[tok] = dict(self.known[e])
        self._mark(tok, reads, writes)
        self.ninst += 1
        return tok

    def dma(self, out, in_, reads=(), writes=(), q="sp"):
        i = self.dnext
        self.dnext = (self.dnext + 1) % self.ndma
        key = ("d", i)
        deps = self._deps(reads, writes)
        if self.cnt[key] > 0:
            deps.append((key, self.cnt[key]))
        self._wait(q, deps)
        inst = self.eng[q].dma_start(out=out, in_=in_)
        self.cnt[key] += 16
        inst.then_inc(self.sem[key], 16)
        tok = (key, self.cnt[key])
        self.hist[tok] = dict(self.known[q])
        self._mark(tok, reads, writes)
        self.ninst += 1
        return tok

    def barrier(self):
        toks = [(("d", i), self.cnt[("d", i)]) for i in range(self.ndma) if self.cnt[("d", i)] > 0]
        for k in ["pe", "act", "dve", "pool"]:
            if self.cnt[k] > 0:
                toks.append((k, self.cnt[k]))
        for e in ["pe", "act", "dve", "pool", "sp"]:
            self._wait(e, [t for t in toks if t[0] != e])

    def finish(self, e="sp"):
        toks = [(("d", i), self.cnt[("d", i)]) for i in range(self.ndma) if self.cnt[("d", i)] > 0]
        for k in ["pe", "act", "dve", "pool"]:
            if self.cnt[k] > 0:
                toks.append((k, self.cnt[k]))
        self._wait(e, toks)

    def mm(self, ob, out, lhsT, rhs, reads, start=True, stop=True):
        return self.op("pe", lambda e: e.matmul(out, lhsT, rhs, start=start, stop=stop),
                       reads=reads, writes=[ob])

    def tr(self, ob, out, in_, ident, reads):
        return self.op("pe", lambda e: e.transpose(out, in_, ident), reads=reads, writes=[ob])

    def tt(self, ob, out, in0, in1, op, reads, eng="dve"):
        return self.op(eng, lambda e: e.tensor_tensor(out=out, in0=in0, in1=in1, op=op),
                       reads=reads, writes=[ob])

    def ts(self, ob, out, in0, s1, op0, reads, s2=None, op1=None, eng="dve"):
        if op1 is None:
            return self.op(eng, lambda e: e.tensor_scalar(out=out, in0=in0, scalar1=s1, scalar2=None, op0=op0),
                           reads=reads, writes=[ob])
        return self.op(eng, lambda e: e.tensor_scalar(out=out, in0=in0, scalar1=s1, scalar2=s2, op0=op0, op1=op1),
                       reads=reads, writes=[ob])

    def stt(self, ob, out, in0, scalar, in1, op0, op1, reads):
        return self.op("dve", lambda e: e.scalar_tensor_tensor(out=out, in0=in0, scalar=scalar, in1=in1,
                                                                 op0=op0, op1=op1),
                       reads=reads, writes=[ob])

    def act(self, ob, out, in_, func, reads, bias=None, scale=1.0, accum_out=None):
        kw = {}
        if bias is not None:
            kw["bias"] = bias
        if accum_out is not None:
            kw["accum_out"] = accum_out
        return self.op("act", lambda e: e.activation(out=out, in_=in_, func=func, scale=scale, **kw),
                       reads=reads, writes=[ob])

    def cp(self, ob, out, in_, reads, eng=None):
        if eng is None:
            eng = ("act", "dve")[self.rr % 2]
            self.rr += 1
        if eng == "act":
            return self.op("act", lambda e: e.copy(out, in_), reads=reads, writes=[ob])
        return self.op(eng, lambda e: e.tensor_copy(out=out, in_=in_), reads=reads, writes=[ob])


def _fm(v):
    v = np.asarray(v, np.float32).reshape(-1)
    return np.ascontiguousarray(v.reshape(-1, 128).T)


PAR_LAYOUT = [("c", 8), ("adab0", 48), ("adab1", 48), ("mix", 48), ("w0", 8), ("a0", 8), ("kk", 8),
              ("ka", 8), ("rk", 8), ("gnw", 8), ("gnb", 8), ("cw", 32), ("cb", 8), ("fw", 8)]
PAR_OFF = {}
_o = 0
for _n, _w in PAR_LAYOUT:
    PAR_OFF[_n] = _o
    _o += _w
NPAR = _o

PART_LAYOUT = [("hnw", 1024), ("bi", 8), ("bf", 8)]
PART_OFF = {}
_o = 0
for _n, _w in PART_LAYOUT:
    PART_OFF[_n] = _o
    _o += _w
NPART = _o


def pack_par(inp, b):
    cols = [
        _fm(inp["c"][b]), _fm(inp["ada_b"][0]), _fm(inp["ada_b"][1]),
        np.concatenate([_fm(inp["rw_mix"][0, i]) for i in range(6)], axis=1),
        _fm(inp["rw_w0"][0]), _fm(inp["rw_a0"][0]), _fm(inp["rw_k_k"][0]), _fm(inp["rw_k_a"][0]),
        _fm(inp["rw_r_k"][0]), _fm(inp["rw_gn_w"][0]), _fm(inp["rw_gn_b"][0]),
        np.concatenate([_fm(inp["ml_conv_w"][0, j]) for j in range(4)], axis=1),
        _fm(inp["ml_conv_b"][0]), _fm(inp["final_w"]),
    ]
    out = np.concatenate(cols, axis=1).astype(np.float32)
    assert out.shape == (128, NPAR)
    return np.ascontiguousarray(out)


def pack_part(inp):
    row = np.concatenate([np.asarray(inp["ml_hn_w"][0], np.float32).reshape(-1),
                          np.asarray(inp["ml_b_i"][0], np.float32).reshape(-1),
                          np.asarray(inp["ml_b_f"][0], np.float32).reshape(-1)])
    return np.ascontiguousarray(np.broadcast_to(row[None, :], (128, NPART))).astype(np.float32)


def make_consts(TT):
    i = np.arange(128)
    ident = np.eye(128, dtype=np.float32)
    m_su = (i[:, None] < i[None, :]).astype(np.float32)
    m_sl = (i[:, None] > i[None, :]).astype(np.float32)
    m_iu = (i[:, None] <= i[None, :]).astype(np.float32)
    ones = np.ones((128, 128), np.float32)
    seg = np.ones((128, TT), np.float32)
    seg[:, ::L] = 0.0
    eps = np.zeros((128, 8), np.float32)
    eps[:, 0] = EPS
    eps[:, 1] = GN_EPS
    rep4 = lambda a: np.concatenate([a] * 4, axis=1)
    cstF = np.concatenate([ident, m_su, m_sl, m_iu, ones, seg, eps, rep4(ident), rep4(m_su), rep4(m_sl), rep4(m_iu)],
                          axis=1)
    blk = (i[:, None] // 64 == i[None, :] // 64).astype(np.float32)
    cstR = np.concatenate([blk / 64.0, blk, ones / 1024.0], axis=1)
    return np.ascontiguousarray(cstF), np.ascontiguousarray(cstR)


def build_nc(TT=256, n_tiles=None, stages=("l0m", "l0f", "l1m", "l1f", "fin"), NSLOT=4):
    ARENA_WORDS = 12288 * TT // 256
    NCH = TT // L
    DBG = os.environ.get('KDBG', '').split(',')
    if n_tiles is None:
        n_tiles = S // TT
    nc = bass.Bass("TRN2", target_bir_lowering=False)
    nc.dge_precook = False
    dram = lambda n, sh, k="ExternalInput": nc.dram_tensor(n, list(sh), F32, kind=k).ap()
    xT = dram("xT", [D, S])
    par_d = dram("par", [P, NPAR])
    part_d = dram("parT", [P, NPART])
    NCF = 5 * 128 + TT + 8 + 4 * 512
    cstF_d = dram("cstF", [P, NCF])
    cstR_d = dram("cstR", [P, 3 * 128])
    ada_w = dram("ada_w", [2, D, 6 * D])
    rw_w_in = dram("rw_w_in", [D, RW_IN])
    rw_w2 = dram("rw_w2", [64, D])
    rw_a2 = dram("rw_a2", [64, D])
    rw_g2 = dram("rw_g2", [160, D])
    rw_w_out = dram("rw_w_out", [D, D])
    ml_w_in = dram("ml_w_in", [D, ML_IN])
    ml_w_out = dram("ml_w_out", [D, D])
    w_gu = dram("ffn_w_gu", [2, D, 2 * DFF])
    w_dn = dram("ffn_w_down", [2, DFF, D])
    outT = dram("outT", [D, S], "ExternalOutput")

    with contextlib.ExitStack() as st:
        kb = KB(nc, st)
        cstF = kb.sb("cstF", [P, NCF])
        cstR = kb.sb("cstR", [P, 3 * 128], F32R)
        par = kb.sb("par", [P, NPAR])
        parT = kb.sb("parT", [P, NPART])
        kb.dma(cstF[:], cstF_d, writes=[cstF])
        kb.dma(cstR[:], cstR_d.bitcast(F32R), writes=[cstR])
        kb.dma(par[:], par_d, writes=[par])
        kb.dma(parT[:], part_d, writes=[parT])
        ident = cstF[:, 0:128]
        m_su = cstF[:, 128:256]
        m_sl = cstF[:, 256:384]
        m_iu = cstF[:, 384:512]
        ones = cstF[:, 512:640]
        segm = cstF[:, 640:640 + TT]
        eps_c = cstF[:, 640 + TT:640 + TT + 1]
        gneps_c = cstF[:, 640 + TT + 1:640 + TT + 2]
        c4 = 640 + TT + 8
        ident4 = cstF[:, c4:c4 + 512]
        m_su4 = cstF[:, c4 + 512:c4 + 1024]
        m_sl4 = cstF[:, c4 + 1024:c4 + 1536]
        m_iu4 = cstF[:, c4 + 1536:c4 + 2048]
        bones64 = cstR[:, 0:128]
        bones1 = cstR[:, 128:256]
        onesD = cstR[:, 256:384]
        pc = lambda n, j=0, w=1: par[:, PAR_OFF[n] + j:PAR_OFF[n] + j + w]

        der = kb.sb("der", [P, 2 * 48 + 2 * 16 + 8 + 8])
        modc = lambda i, j, w=1: der[:, i * 48 + j:i * 48 + j + w]
        sc1p = lambda i, c: der[:, 96 + i * 16 + c:96 + i * 16 + c + 1]
        sc2p = lambda i, c: der[:, 96 + i * 16 + 8 + c:96 + i * 16 + 8 + c + 1]
        oka = lambda m: der[:, 128 + m:128 + m + 1]
        cact = lambda kc: der[:, 136 + kc:136 + kc + 1]

        SLOTW = 2048
        slots_t = kb.tile("wslots", [P, NSLOT * SLOTW], F32R)
        slots = [Buf("ws%d" % i, slots_t[:, i * SLOTW:(i + 1) * SLOTW]) for i in range(NSLOT)]
        slot_i = [0]

        def wslot():
            s_ = slots[slot_i[0] % NSLOT]
            slot_i[0] += 1
            return s_

        def wslot2():
            if slot_i[0] % 2:
                slot_i[0] += 1
            i0 = slot_i[0] % NSLOT
            slot_i[0] += 2
            return [slots[i0], slots[i0 + 1]], slots_t[:, i0 * SLOTW:(i0 + 2) * SLOTW]

        NROT = 7
        banks = [kb.ps("pb%d" % i, [P, 512]) for i in range(NROT)]
        ypsum = kb.ps("ypsum", [P, 512])
        bank_i = [0]

        def pbank():
            b_ = banks[bank_i[0] % NROT]
            bank_i[0] += 1
            return b_

        ARW = ARENA_WORDS
        arena_t = kb.tile("arena", [P, ARW])
        ARWR = 6144 * TT // 256
        arenaR_t = kb.tile("arenaR", [P, ARWR], F32R)
        aoff = [0, 0]
        aoffR = [0]

        def aalloc(shape, dt=F32):
            words = 1
            for d_ in shape[1:]:
                words *= d_
            words += words % 2
            if dt == F32R:
                ap = arenaR_t[0:shape[0], aoffR[0]:aoffR[0] + words]
                aoffR[0] += words
                assert aoffR[0] <= ARWR, (aoffR[0], ARWR)
            else:
                ap = arena_t[0:shape[0], aoff[0]:aoff[0] + words]
                aoff[0] += words
                aoff[1] = max(aoff[1], aoff[0])
                assert aoff[0] <= ARW, (aoff[0], ARW)
            if len(shape) == 3:
                ap = ap.rearrange("p (a b) -> p a b", a=shape[1])
            return ap

        def asb(name, shape, dt=F32):
            return Buf(name, aalloc(shape, dt))

        x_t = kb.tile("x", [P, KC, 2 * TT])
        xb = [Buf("x%d" % c, x_t[:, c, :]) for c in range(KC)]
        XO = [0]
        h_t = kb.tile("h", [P, KC, TT + 1], F32R)
        hb = [Buf("h%d" % c, h_t[:, c, :]) for c in range(KC)]
        yg_t = kb.tile("yg", [P, KC, TT], F32R)
        ygb = [Buf("yg%d" % c, yg_t[:, c, :]) for c in range(KC)]
        xs_all = kb.tile("xs", [P, KC, 3 * TT], F32R)
        xsb = [[Buf("xs%d_%d" % (i, c), xs_all[:, c, i * TT:(i + 1) * TT]) for c in range(KC)] for i in range(3)]
        h2b = [Buf("h2_%d" % c, xs_all[:, c, 0:2 * TT]) for c in range(KC)]
        sq = [Buf("sq%d" % i, None) for i in range(2)]
        rs = kb.sb("rs", [P, TT])
        tmpn = [kb.sb("tmpn%d" % i, [P, TT]) for i in range(2)]

        for c in range(KC):
            kb.ts(hb[c], hb[c][:, 0:1], cstF[:, 0:1], 0.0, ALU.mult, [cstF])

        kb.act(der, der[:, 136:144], pc("c", 0, 8), AF.Silu, [par])
        for kc in range(KC):
            kb.ts(h2b[kc], h2b[kc][:, 0:128], cstF[:, 0:128], 0.0, ALU.mult, [cstF])
            kb.cp(h2b[kc], h2b[kc][:, 0:1], cact(kc), [der], eng="act")
        rowb = asb("rowb", [1, 512])
        for i in range(2):
            mp = ypsum
            for blk in range(12):
                wbufs, wap = wslot2()
                wvr = wap.rearrange("p (k n) -> p k n", k=KC)
                kb.dma(wvr, ada_w[i].rearrange("(k p) n -> p k n", p=P)[:, :, blk * 512:(blk + 1) * 512].bitcast(F32R),
                       writes=wbufs)
                pr_ = pbank()
                for kc in range(KC):
                    kb.mm(pr_, pr_[:, 0:512], h2b[kc][:, 0:128], wvr[:, kc, :], wbufs + [h2b[kc]],
                          start=(kc == 0), stop=(kc == KC - 1))
                kb.cp(rowb, rowb[0:1, :], pr_[0:1, 0:512], [pr_], eng="act")
                for jj in range(4):
                    j = blk * 4 + jj
                    kb.mm(mp, mp[:, j:j + 1], rowb[0:1, jj * 128:(jj + 1) * 128], ones[0:1, 0:1], [rowb, cstF])
            kb.tt(der, der[:, i * 48:(i + 1) * 48], mp[:, 0:48], pc("adab%d" % i, 0, 48), ALU.add, [mp, par])
            kb.ts(der, der[:, 96 + i * 16:96 + i * 16 + 8], modc(i, 8, 8), 1.0, ALU.add, [der])
            kb.ts(der, der[:, 96 + i * 16 + 8:96 + i * 16 + 16], modc(i, 32, 8), 1.0, ALU.add, [der])
        kb.ts(der, der[:, 128:136], pc("ka", 0, 8), -1.0, ALU.mult, [par], s2=1.0, op1=ALU.add)

        Hst = [kb.sb("H%d" % m, [P, 128]) for m in range(8)]
        Cst = [kb.sb("C%d" % m, [P, 129]) for m in range(4)]
        for b_ in Hst + Cst:
            kb.op("pool", lambda e: e.memset(b_[:, :], 0.0), writes=[b_])
        qkpre_t = kb.tile("qkpre", [P, KC, TT + 3])
        qkpre = [Buf("qkpre%d" % c, qkpre_t[:, c, :]) for c in range(KC)]
        kb.op("pool", lambda e: e.memset(qkpre_t[:, :, 0:3], 0.0), writes=qkpre)

        lw2 = [asb("lw2_%d" % i, [64, 128], F32R) for i in range(2)]
        la2 = [asb("la2_%d" % i, [64, 128], F32R) for i in range(2)]
        lg2 = [asb("lg2_%d" % i, [P, 2, 128], F32R) for i in range(2)]

        def load_lora(m):
            msl_ = slice(m * 128, (m + 1) * 128)
            kb.dma(lw2[m % 2][:, :], rw_w2[:, msl_].bitcast(F32R), writes=[lw2[m % 2]])
            kb.dma(la2[m % 2][:, :], rw_a2[:, msl_].bitcast(F32R), writes=[la2[m % 2]])
            kb.dma(lg2[m % 2][0:96, 0, :], rw_g2[0:96, msl_].bitcast(F32R), writes=[lg2[m % 2]])
            kb.dma(lg2[m % 2][64:128, 1, :], rw_g2[96:160, msl_].bitcast(F32R), writes=[lg2[m % 2]])

        def wk(name, n=1, dt=F32, shape=None):
            return [asb("%s%d" % (name, i), shape or [P, TT], dt) for i in range(n)]

        def normmod(i, scp, shoff, outb, outoff):
            ssb = pbank()
            for c in range(KC):
                s_ = sq[c % 2]
                kb.act(s_, s_[:, :], xb[c][:, XO[0]:XO[0] + TT], AF.Square, [xb[c]])
                kb.mm(ssb, ssb[:, 0:TT], onesD, s_[:, :], [s_, cstR], start=(c == 0), stop=(c == KC - 1))
            kb.act(rs, rs[:, :], ssb[:, 0:TT], AF.Ln, [ssb, cstF], bias=eps_c)
            kb.act(rs, rs[:, :], rs[:, :], AF.Exp, [rs], scale=-0.5)
            for c in range(KC):
                t_ = tmpn[c % 2]
                kb.stt(t_, t_[:, :], xb[c][:, XO[0]:XO[0] + TT], scp(i, c), rs[:, :], ALU.mult, ALU.mult, [xb[c], rs, der])
                kb.act(outb[c], outb[c][:, outoff:outoff + TT], t_[:, :], AF.Identity, [t_, der],
                       bias=modc(i, shoff + c))

        def outproj(i, w_d, srcb):
            for op_ in range(KC // 2):
                ws = wslot()
                wv = ws[:, 0:KC * 256].rearrange("p (k n) -> p k n", k=KC)
                kb.dma(wv, w_d.rearrange("(k p) n -> p k n", p=P)[:, :, op_ * 256:(op_ + 1) * 256].bitcast(F32R),
                       writes=[ws])
                for oo in range(2):
                    o = op_ * 2 + oo
                    pb = pbank()
                    for kc in range(KC):
                        kb.mm(pb, pb[:, 0:TT], wv[:, kc, oo * 128:(oo + 1) * 128], srcb[kc][:, 0:TT], [ws, srcb[kc]],
                              start=(kc == 0), stop=(kc == KC - 1))
                    kb.stt(xb[o], xb[o][:, XO[0]:XO[0] + TT], pb[:, 0:TT], modc(i, 16 + o),
                           xb[o][:, XO[0]:XO[0] + TT], ALU.mult, ALU.add, [pb, xb[o], der])

        TP = 2 * TT
        for i_ in range(2):
            sq[i_].ap = aalloc([P, TT], F32R)
        hff_t = aalloc([P, 4, TP], F32R)
        hff = [Buf("hff%d" % j, hff_t[:, j, :]) for j in range(4)]
        sgb = [asb("sgb%d" % i, [P, TP]) for i in range(2)]
        aoff[0] = 0

        def ffn(i):
            kb.barrier()
            for s_i in range(2):
                XO[0] = s_i * TT
                normmod(i, sc2p, 24, h2b, s_i * TT)
            ngroups = (DFF // 128 + 3) // 4
            for g in range(ngroups):
                nj = min(4, DFF // 128 - g * 4)
                src = w_gu[i].rearrange("(k p) n -> p k n", p=P)
                for jp in range(nj // 2):
                    f = g * 4 + jp * 2
                    wsA = wslot()
                    wvA = wsA[:, 0:KC * 256].rearrange("p (k n) -> p k n", k=KC)
                    kb.dma(wvA, src[:, :, f * 128:(f + 2) * 128].bitcast(F32R), writes=[wsA])
                    wsB = wslot()
                    wvB = wsB[:, 0:KC * 256].rearrange("p (k n) -> p k n", k=KC)
                    kb.dma(wvB, src[:, :, DFF + f * 128:DFF + (f + 2) * 128].bitcast(F32R), writes=[wsB])
                    for jj in range(2):
                        j = jp * 2 + jj
                        csl = slice(jj * 128, (jj + 1) * 128)
                        pg = pbank()
                        for kc in range(KC):
                            kb.mm(pg, pg[:, 0:TP], wvA[:, kc, csl], h2b[kc][:, :], [wsA, h2b[kc]],
                                  start=(kc == 0), stop=(kc == KC - 1))
                        pu = pbank()
                        for kc in range(KC):
                            kb.mm(pu, pu[:, 0:TP], wvB[:, kc, csl], h2b[kc][:, :], [wsB, h2b[kc]],
                                  start=(kc == 0), stop=(kc == KC - 1))
                        s_ = sgb[j % 2]
                        kb.act(s_, s_[:, :], pg[:, 0:TP], AF.Silu, [pg])
                        kb.tt(hff[j], hff[j][:, :], s_[:, :], pu[:, 0:TP], ALU.mult, [s_, pu])
                for half in range(2):
                    ws = wslot()
                    wv = ws[:, 0:nj * 512].rearrange("p (k n) -> p k n", k=nj)
                    kb.dma(wv, w_dn[i][g * 512:g * 512 + nj * 128, :].rearrange("(k p) n -> p k n", p=P)[
                        :, :, half * 512:(half + 1) * 512].bitcast(F32R), writes=[ws])
                    for oo in range(4):
                        o = half * 4 + oo
                        pb = pbank()
                        for j in range(nj):
                            kb.mm(pb, pb[:, 0:TP], wv[:, j, oo * 128:(oo + 1) * 128], hff[j][:, :],
                                  [ws, hff[j]], start=(j == 0), stop=(j == nj - 1))
                        kb.stt(xb[o], xb[o][:, :], pb[:, 0:TP], modc(i, 40 + o), xb[o][:, :], ALU.mult, ALU.add,
                               [pb, xb[o], der])

        wl = asb("wl", [64, TT], F32R)
        al = asb("al", [64, TT], F32R)
        gl = asb("gl", [P, 2, TT], F32R)

        def wk(name, n=1, dt=F32, shape=None):
            return [asb("%s%d" % (name, i), shape or [P, TT], dt) for i in range(n)]

        r_b, k0_b, k_b = wk("r")[0], wk("k0")[0], wk("k")[0]
        lw_b, cl_b, asg_b = wk("lw")[0], wk("cl")[0], wk("asg")[0]
        kkn_b, b_b = wk("kkn")[0], wk("bb")[0]
        kksq_b = wk("kksq", 1, F32R)[0]
        e2_b, e3_b = wk("e2")[0], wk("e3")[0]
        t1_b, t2_b = wk("t1")[0], wk("t2")[0]
        rkp_b = wk("rkp", 1, F32R)[0]
        DB = []
        for i_ in range(2):
            d_ = {}
            for nm in ("At", "Rt", "v", "Bp", "Kp", "e1", "bon", "g"):
                d_[nm] = asb("%s_%d" % (nm, i_), [P, TT])
            for nm in ("Atb", "Btb", "Ktb", "Rtb"):
                d_[nm] = kb.sb("%s_%d" % (nm, i_), [P, TT], BF16)
            d_["ys"] = asb("ys_%d" % i_, [P, TT], F32R)
            DB.append(d_)
        yc_b = wk("yc")[0]
        ysq_b = wk("ysq", 1, F32R)[0]
        t1c_b = wk("t1c")[0]
        sm = lambda name, n, w=128: [asb("%s%d" % (name, i), [P, w]) for i in range(n)]
        NCHAIN = 4
        big = lambda name, w=NCHAIN * 128: asb(name, [P, w])
        S1 = [big("S1_%d" % i) for i in range(2)]
        S2 = [big("S2_%d" % i) for i in range(2)]
        bfb = lambda name, w=NCHAIN * 128: kb.sb(name, [P, w], BF16)
        S3 = [bfb("S3_%d" % i, 256) for i in range(2)]
        P0b = [bfb("P0b_%d" % i, 256) for i in range(2)]
        PmA, PTmA, PmB, PTmB = bfb("PmA"), bfb("PTmA"), bfb("PmB"), bfb("PTmB")
        QmA = big("QmA")
        Qbf = bfb("Qbf")
        TM3 = [asb("TM3_%d" % i, [P, 3 * 128]) for i in range(2)]
        Zs, Us = sm("Zs", 2, 128), sm("Us", 2, 128)

        def inproj_fm(w_d, col0, n, rhs_list, pb, mix=None):
            ws = wslot()
            nk = len(rhs_list)
            wv = ws[:, 0:KC * 128].rearrange("p (k n) -> p k n", k=KC)
            kb.dma(wv[:, 0:KC, 0:n], w_d.rearrange("(k p) n -> p k n", p=P)[:, :, col0:col0 + n].bitcast(F32R),
                   writes=[ws])
            for kk_ in range(nk):
                rb, rap = rhs_list[kk_]
                kb.mm(pb, pb[0:n, 0:TT], wv[:, kk_, 0:n], rap, [ws, rb], start=(kk_ == 0), stop=(kk_ == nk - 1))

        def mkxs(mix, dst):
            for c in range(KC):
                kb.stt(dst[c], dst[c][:, 0:TT], qkpre[c][:, 3:3 + TT], pc("mix", mix * 8 + c),
                       hb[c][:, 1:TT + 1].bitcast(F32), ALU.mult, ALU.add, [qkpre[c], hb[c], par])
            return [(dst[c], dst[c][:, 0:TT]) for c in range(KC)]

        def stage_A(m, x0, x2, x3):
            D_ = DB[m % 2]
            At_b, Rt_b, v_b, Bp_b, Kp_b, e1_b, bon_b, g_b = (D_["At"], D_["Rt"], D_["v"], D_["Bp"], D_["Kp"],
                                                                   D_["e1"], D_["bon"], D_["g"])
            Atb, Btb, Ktb, Rtb = D_["Atb"], D_["Btb"], D_["Ktb"], D_["Rtb"]
            if m + 1 < 8:
                load_lora(m + 1)
            pr = pbank()
            inproj_fm(rw_w_in, m * 128, 128, x0, pr)
            kb.cp(r_b, r_b[:, :], pr[:, 0:TT], [pr], eng="act")
            yield
            pk = pbank()
            inproj_fm(rw_w_in, 1088 + m * 128, 128, x2, pk)
            kb.cp(k0_b, k0_b[:, :], pk[:, 0:TT], [pk], eng="dve")
            yield
            pv = pbank()
            inproj_fm(rw_w_in, 2112 + m * 128, 128, x3, pv)
            kb.cp(v_b, v_b[:, :], pv[:, 0:TT], [pv], eng="act")
            yield
            pw = pbank()
            kb.mm(pw, pw[:, 0:TT], lw2[m % 2][:, :], wl[:, :], [lw2[m % 2], wl])
            kb.act(lw_b, lw_b[:, :], pw[:, 0:TT], AF.Sigmoid, [pw, par], bias=pc("w0", m))
            pa = pbank()
            kb.mm(pa, pa[:, 0:TT], la2[m % 2][:, :], al[:, :], [la2[m % 2], al])
            kb.act(asg_b, asg_b[:, :], pa[:, 0:TT], AF.Sigmoid, [pa, par], bias=pc("a0", m))
            yield
            kb.ts(lw_b, lw_b[:, :], lw_b[:, :], -0.6065306597126334, ALU.mult, [lw_b], eng="dve")
            pg_ = pbank()
            kb.mm(pg_, pg_[:, 0:TT], lg2[m % 2][0:96, 0, :], gl[0:96, 0, :], [lg2[m % 2], gl], start=True, stop=False)
            kb.mm(pg_, pg_[:, 0:TT], lg2[m % 2][64:128, 1, :], gl[64:128, 1, :], [lg2[m % 2], gl], start=False, stop=True)
            kb.cp(g_b, g_b[:, :], pg_[:, 0:TT], [pg_], eng="act")
            yield
            kb.ts(kkn_b, kkn_b[:, :], k0_b[:, :], pc("kk", m), ALU.mult, [k0_b, par])
            kb.act(kksq_b, kksq_b[:, :], kkn_b[:, :], AF.Square, [kkn_b])
            pn = pbank()
            kb.mm(pn, pn[:, 0:TT], bones1, kksq_b[:, :], [cstR, kksq_b])
            kb.ts(t1_b, t1_b[:, :], pn[:, 0:TT], 1e-18, ALU.max, [pn])
            yield
            kb.op("dve", lambda e: e.tensor_tensor_scan(out=cl_b[:, :], data0=segm, data1=lw_b[:, :],
                                                          initial=0.0, op0=ALU.mult, op1=ALU.add),
                  reads=[cstF, lw_b], writes=[cl_b])
            yield
            kb.act(t1_b, t1_b[:, :], t1_b[:, :], AF.Ln, [t1_b])
            kb.act(t1_b, t1_b[:, :], t1_b[:, :], AF.Exp, [t1_b], scale=-0.5)
            yield
            kb.tt(kkn_b, kkn_b[:, :], kkn_b[:, :], t1_b[:, :], ALU.mult, [kkn_b, t1_b])
            kb.ts(t2_b, t2_b[:, :], asg_b[:, :], pc("ka", m), ALU.mult, [asg_b, par, der], s2=oka(m), op1=ALU.add)
            yield
            kb.tt(k_b, k_b[:, :], k0_b[:, :], t2_b[:, :], ALU.mult, [k0_b, t2_b])
            kb.tt(b_b, b_b[:, :], kkn_b[:, :], asg_b[:, :], ALU.mult, [kkn_b, asg_b], eng="pool")
            yield
            kb.act(e1_b, e1_b[:, :], cl_b[:, :], AF.Exp, [cl_b])
            kb.act(e2_b, e2_b[:, :], cl_b[:, :], AF.Exp, [cl_b], scale=-1.0)
            kb.tt(t1_b, t1_b[:, :], cl_b[:, :], lw_b[:, :], ALU.subtract, [cl_b, lw_b], eng="pool")
            yield
            kb.act(e3_b, e3_b[:, :], t1_b[:, :], AF.Exp, [t1_b])
            kb.tt(Btb, Btb[:, :], b_b[:, :], e2_b[:, :], ALU.mult, [b_b, e2_b])
            kb.tt(Ktb, Ktb[:, :], k_b[:, :], e2_b[:, :], ALU.mult, [k_b, e2_b], eng="pool")
            yield
            kb.tt(Rt_b, Rt_b[:, :], r_b[:, :], e1_b[:, :], ALU.mult, [r_b, e1_b])
            kb.cp(Rtb, Rtb[:, :], Rt_b[:, :], [Rt_b], eng="pool")
            kb.stt(At_b, At_b[:, :], kkn_b[:, :], -1.0, e3_b[:, :], ALU.mult, ALU.mult, [kkn_b, e3_b])
            kb.cp(Atb, Atb[:, :], At_b[:, :], [At_b], eng="pool")
            yield
            for c in range(NCH):
                cs = slice(c * L, (c + 1) * L)
                kb.act(e3_b, e3_b[:, cs], cl_b[:, cs], AF.Exp, [cl_b], scale=-1.0,
                       bias=cl_b[:, (c + 1) * L - 1:(c + 1) * L])
            yield
            kb.tt(Bp_b, Bp_b[:, :], b_b[:, :], e3_b[:, :], ALU.mult, [b_b, e3_b])
            kb.tt(Kp_b, Kp_b[:, :], k_b[:, :], e3_b[:, :], ALU.mult, [k_b, e3_b], eng="pool")
            yield
            kb.stt(rkp_b, rkp_b[:, :], r_b[:, :], pc("rk", m), k_b[:, :], ALU.mult, ALU.mult, [r_b, k_b, par])
            pbn = pbank()
            kb.mm(pbn, pbn[:, 0:TT], bones1, rkp_b[:, :], [cstR, rkp_b])
            kb.tt(bon_b, bon_b[:, :], pbn[:, 0:TT], v_b[:, :], ALU.mult, [pbn, v_b])
            yield

        def stage_B(m):
            D_ = DB[m % 2]
            At_b, Rt_b, v_b, Bp_b, Kp_b, e1_b = D_["At"], D_["Rt"], D_["v"], D_["Bp"], D_["Kp"], D_["e1"]
            Atb, Btb, Ktb, Rtb, ysb = D_["Atb"], D_["Btb"], D_["Ktb"], D_["Rtb"], D_["ys"]
            H = Hst[m]
            for g0 in range(0, NCH, 2):
                gchunks = list(range(g0, min(g0 + 2, NCH)))
                ncg = len(gchunks)
                for c in gchunks:
                    cs = slice(c * L, (c + 1) * L)
                    pt = pbank()
                    for j_, src in enumerate((v_b, Bp_b, Kp_b)):
                        kb.tr(pt, pt[:, j_ * 128:(j_ + 1) * 128], src[:, cs], ident, [src, cstF])
                    kb.cp(TM3[c % 2], TM3[c % 2][:, :], pt[:, 0:384], [pt])
                yield
                for hh in range(2):
                    hs = slice(hh * 64, (hh + 1) * 64)
                    for (pairs, dst, msk4) in ((((Btb, Atb), (Ktb, Atb)), S1[hh], m_su4),
                                               (((Btb, Rtb), (Ktb, Rtb)), S2[hh], m_iu4),
                                               (((Atb, Btb),), S3[hh], m_sl4)):
                        pp = pbank()
                        W_ = len(pairs) * 256
                        for j_, (lb, rb_) in enumerate(pairs):
                            for ci, c in enumerate(gchunks):
                                cs = slice(c * L, (c + 1) * L)
                                o_ = j_ * 256 + ci * 128
                                kb.mm(pp, pp[:, o_:o_ + 128], lb[hs, cs], rb_[hs, cs], [lb, rb_])
                        if ncg == 2:
                            kb.tt(dst, dst[:, 0:W_], pp[:, 0:W_], msk4[:, 0:W_], ALU.mult, [pp, cstF])
                        else:
                            for j_ in range(len(pairs)):
                                kb.tt(dst, dst[:, j_ * 256:j_ * 256 + 128], pp[:, j_ * 256:j_ * 256 + 128],
                                      msk4[:, 0:128], ALU.mult, [pp, cstF])
                        yield
                chains = [(hh, ci) for hh in range(2) for ci in range(ncg)]
                ks = lambda k_: slice(k_ * 128, (k_ + 1) * 128)
                kq = lambda hh, ci: hh * 2 + ci
                Pc = [(P0b[hh], slice(ci * 128, (ci + 1) * 128)) for hh, ci in chains]
                PTc = [(S3[hh], slice(ci * 128, (ci + 1) * 128)) for hh, ci in chains]
                for hh in range(2):
                    kb.tt(QmA, QmA[:, hh * 256:hh * 256 + ncg * 128], S1[hh][:, 0:ncg * 128],
                          ident4[:, 0:ncg * 128], ALU.add, [S1[hh], cstF], eng="pool")
                    kb.cp(P0b[hh], P0b[hh][:, 0:ncg * 128], S1[hh][:, 0:ncg * 128], [S1[hh]], eng="act")
                kb.cp(Qbf, Qbf[:, :], QmA[:, :], [QmA], eng="dve")
                yield
                for it in range(1, 7):
                    last = (it == 6)
                    b1 = pbank()
                    for (hh, ci), (pb_, psl), (ptb_, ptsl) in zip(chains, Pc, PTc):
                        kb.mm(b1, b1[:, ks(kq(hh, ci))], pb_[:, psl], ptb_[:, ptsl], [pb_, ptb_])
                    if not last:
                        b2 = pbank()
                        for (hh, ci), (pb_, psl), (ptb_, ptsl) in zip(chains, Pc, PTc):
                            kb.mm(b2, b2[:, ks(kq(hh, ci))], ptb_[:, ptsl], pb_[:, psl], [pb_, ptb_])
                    nPT = PTmA if (it % 2 == 1) else PTmB
                    kb.cp(nPT, nPT[:, :], b1[:, :], [b1], eng="act")
                    PTc = [(nPT, ks(kq(hh, ci))) for hh, ci in chains]
                    if not last:
                        nP = PmA if (it % 2 == 1) else PmB
                        kb.cp(nP, nP[:, :], b2[:, :], [b2], eng="act")
                        Pc = [(nP, ks(kq(hh, ci))) for hh, ci in chains]
                    yield
                    b3 = pbank()
                    for (hh, ci), (ptb_, ptsl) in zip(chains, PTc):
                        kb.mm(b3, b3[:, ks(kq(hh, ci))], ptb_[:, ptsl], Qbf[:, ks(kq(hh, ci))], [ptb_, Qbf])
                    if not last:
                        kb.tt(Qbf, Qbf[:, :], b3[:, :], QmA[:, :], ALU.add, [b3, QmA])
                    kb.tt(QmA, QmA[:, :], b3[:, :], QmA[:, :], ALU.add, [b3, QmA])
                    yield
                for ci, c in enumerate(gchunks):
                    cs = slice(c * L, (c + 1) * L)
                    tm = TM3[c % 2]
                    pz = pbank()
                    kb.mm(pz, pz[:, 0:128], At_b[:, cs], H[:, :], [At_b, H], start=True, stop=False)
                    for hh in range(2):
                        vs = slice(hh * 64, (hh + 1) * 64)
                        kb.mm(pz, pz[:, vs], S1[hh][:, 256 + ci * 128:256 + (ci + 1) * 128], tm[:, vs],
                              [S1[hh], tm], start=False, stop=(hh == 1))
                    zsb = Zs[c % 2]
                    kb.cp(zsb, zsb[:, :], pz[:, 0:128], [pz], eng="act")
                    yield
                    pu_ = pbank()
                    for hh in range(2):
                        vs = slice(hh * 64, (hh + 1) * 64)
                        kb.mm(pu_, pu_[:, vs], QmA[:, ks(kq(hh, ci))], zsb[:, vs], [QmA, zsb])
                    usb = Us[c % 2]
                    kb.cp(usb, usb[:, :], pu_[:, 0:128], [pu_], eng="dve")
                    yield
                    kb.mm(ypsum, ypsum[:, cs], H[:, :], Rt_b[:, cs], [H, Rt_b], start=True, stop=False)
                    for hh in range(2):
                        hs = slice(hh * 64, (hh + 1) * 64)
                        vs = slice(hh * 64, (hh + 1) * 64)
                        yo = ypsum[hs, cs]
                        kb.mm(ypsum, yo, usb[:, vs], S2[hh][:, ci * 128:(ci + 1) * 128], [usb, S2[hh]],
                              start=False, stop=False)
                        kb.mm(ypsum, yo, tm[:, vs], S2[hh][:, 256 + ci * 128:256 + (ci + 1) * 128], [tm, S2[hh]],
                              start=False, stop=True)
                    ph = pbank()
                    for hh in range(2):
                        hs = slice(hh * 64, (hh + 1) * 64)
                        vs = slice(hh * 64, (hh + 1) * 64)
                        kb.mm(ph, ph[hs, 0:64], tm[:, 128 + hh * 64:128 + (hh + 1) * 64], usb[:, vs], [tm, usb],
                              start=True, stop=False)
                        kb.mm(ph, ph[hs, 0:64], tm[:, 256 + hh * 64:256 + (hh + 1) * 64], tm[:, vs], [tm],
                              start=False, stop=True)
                    for hh in range(2):
                        hs = slice(hh * 64, (hh + 1) * 64)
                        vs = slice(hh * 64, (hh + 1) * 64)
                        kb.stt(H, H[hs, vs], H[hs, vs], e1_b[hs, (c + 1) * L - 1:(c + 1) * L], ph[hs, 0:64],
                               ALU.mult, ALU.add, [H, e1_b, ph])
                    yield
            kb.cp(ysb, ysb[:, :], ypsum[:, 0:TT], [ypsum], eng="act")
            yield

        def stage_C(m):
            D_ = DB[m % 2]
            ysb, bon_b, g_b = D_["ys"], D_["bon"], D_["g"]
            pm_ = pbank()
            kb.mm(pm_, pm_[:, 0:TT], bones64, ysb[:, :], [cstR, ysb])
            kb.tt(yc_b, yc_b[:, :], ysb[:, :].bitcast(F32), pm_[:, 0:TT], ALU.subtract, [ysb, pm_])
            yield
            kb.act(ysq_b, ysq_b[:, :], yc_b[:, :], AF.Square, [yc_b])
            pv_ = pbank()
            kb.mm(pv_, pv_[:, 0:TT], bones64, ysq_b[:, :], [cstR, ysq_b])
            kb.act(t1c_b, t1c_b[:, :], pv_[:, 0:TT], AF.Ln, [pv_, cstF], bias=gneps_c)
            yield
            kb.act(t1c_b, t1c_b[:, :], t1c_b[:, :], AF.Exp, [t1c_b], scale=-0.5)
            yield
            kb.tt(yc_b, yc_b[:, :], yc_b[:, :], t1c_b[:, :], ALU.mult, [yc_b, t1c_b])
            kb.ts(yc_b, yc_b[:, :], yc_b[:, :], pc("gnw", m), ALU.mult, [yc_b, par], s2=pc("gnb", m), op1=ALU.add)
            yield
            kb.tt(yc_b, yc_b[:, :], yc_b[:, :], bon_b[:, :], ALU.add, [yc_b, bon_b], eng="pool")
            kb.tt(ygb[m], ygb[m][:, :], yc_b[:, :], g_b[:, :], ALU.mult, [yc_b, g_b])
            yield

        def drain(g):
            for _ in g:
                pass

        def l0_mixer(tile_idx):
            kb.barrier()
            load_lora(0)
            normmod(0, sc1p, 0, hb, 1)
            for c in range(KC):
                kb.tt(qkpre[c], qkpre[c][:, 3:3 + TT], hb[c][:, 0:TT].bitcast(F32), hb[c][:, 1:TT + 1].bitcast(F32),
                      ALU.subtract, [hb[c]], eng="pool")
            x1 = mkxs(1, ygb)
            x4 = mkxs(4, xsb[2])
            x5 = mkxs(5, xsb[1])
            pb = pbank()
            inproj_fm(rw_w_in, 1024, 128, x1, pb)
            kb.act(wl, wl[:, :], pb[0:64, 0:TT], AF.Tanh, [pb])
            pb = pbank()
            inproj_fm(rw_w_in, 3136, 128, x4, pb)
            kb.cp(al, al[:, :], pb[0:64, 0:TT], [pb], eng="act")
            pb = pbank()
            inproj_fm(rw_w_in, 3200, 128, x5, pb)
            kb.act(gl, gl[:, 0, :], pb[:, 0:TT], AF.Sigmoid, [pb])
            pb = pbank()
            inproj_fm(rw_w_in, 3232, 128, x5, pb)
            kb.act(gl, gl[:, 1, :], pb[:, 0:TT], AF.Sigmoid, [pb])
            x0 = mkxs(0, xsb[0])
            x2 = mkxs(2, xsb[1])
            x3 = mkxs(3, xsb[2])
            for c in range(KC):
                kb.cp(hb[c], hb[c][:, 0:1], hb[c][:, TT:TT + 1], [hb[c]], eng="pool")
            drain(stage_A(0, x0, x2, x3))
            for m in range(8):
                fills = []
                if m > 0:
                    fills.append(stage_C(m - 1))
                if m + 1 < 8:
                    fills.append(stage_A(m + 1, x0, x2, x3))
                for _ in stage_B(m):
                    for f_ in list(fills):
                        try:
                            next(f_)
                        except StopIteration:
                            fills.remove(f_)
                for f_ in fills:
                    drain(f_)
            drain(stage_C(7))
            outproj(0, rw_w_out, ygb)

        aoff[0] = 0
        qk_t = aalloc([P, KC, TT])
        qkb = [Buf("qk%d" % c, qk_t[:, c, :]) for c in range(KC)]
        vT_t = aalloc([P, NCH, D])
        vTb = [Buf("vT%d" % c, vT_t[:, c, :]) for c in range(NCH)]
        oT_t = aalloc([P, NCH, D])
        oTb = [Buf("oT%d" % c, oT_t[:, c, :]) for c in range(NCH)]
        hgT_t = aalloc([P, NCH, D])
        hgTb = [Buf("hgT%d" % c, hgT_t[:, c, :]) for c in range(NCH)]
        gts = [asb("gts%d" % c, [P, 64]) for c in range(NCH)]
        NHF = 3
        Vg = [asb("Vg%d" % i, [P, 130]) for i in range(NHF)]
        STm = sm("STm", NHF)
        kTm = sm("kTm", 4)
        osb = [asb("osb%d" % i, [P, 136]) for i in range(NHF)]
        hhb = [asb("hhb%d" % i, [P, 128]) for i in range(NHF)]
        ctmp = [asb("ctmp%d" % i, [P, 130]) for i in range(NHF)]
        acc_b = wk("acc")[0]

        def l1_mixer(tile_idx):
            kb.barrier()
            normmod(1, sc1p, 0, hb, 1)
            for m in range(8):
                pb = pbank()
                inproj_fm(ml_w_in, m * 128, 128, [(hb[c], hb[c][:, 1:TT + 1]) for c in range(KC)], pb)
                kb.cp(qkpre[m], qkpre[m][:, 3:3 + TT], pb[:, 0:TT], [pb], eng="act")
                cw = lambda j: pc("cw", j * 8 + m)
                kb.ts(acc_b, acc_b[:, :], qkpre[m][:, 0:TT], cw(0), ALU.mult, [qkpre[m], par])
                for j in range(1, 4):
                    kb.stt(acc_b, acc_b[:, :], qkpre[m][:, j:j + TT], cw(j), acc_b[:, :], ALU.mult, ALU.add,
                           [qkpre[m], par, acc_b])
                kb.act(qkb[m], qkb[m][:, :], acc_b[:, :], AF.Silu, [acc_b, par], bias=pc("cb", m))
                if m < 4:
                    kb.ts(qkb[m], qkb[m][:, :], qkb[m][:, :], 0.125, ALU.mult, [qkb[m]], eng="pool")
                kb.cp(qkpre[m], qkpre[m][:, 0:3], qkpre[m][:, TT:TT + 3], [qkpre[m]], eng="pool")
            for blk in range(5):
                ncol = 512 if blk < 4 else 16
                c0 = 1024 + blk * 512
                wbufs, wap = wslot2()
                wv = wap.rearrange("p (k n) -> p k n", k=KC)
                kb.dma(wv[:, :, 0:ncol],
                       ml_w_in.rearrange("(k p) n -> p k n", p=P)[:, :, c0:c0 + ncol].bitcast(F32R),
                       writes=wbufs)
                for tb in range(NCH):
                    lhs = [(hb[kc], hb[kc][:, 1 + tb * L:1 + (tb + 1) * L]) for kc in range(KC)]
                    pb = pbank()
                    for kc in range(KC):
                        kb.mm(pb, pb[:, 0:ncol], lhs[kc][1], wv[:, kc, 0:ncol], wbufs + [lhs[kc][0]],
                              start=(kc == 0), stop=(kc == KC - 1))
                    if blk < 2:
                        kb.cp(vTb[tb], vTb[tb][:, blk * 512:(blk + 1) * 512], pb[:, 0:512], [pb])
                    elif blk < 4:
                        kb.act(oTb[tb], oTb[tb][:, (blk - 2) * 512:(blk - 1) * 512], pb[:, 0:512], AF.Sigmoid, [pb])
                    else:
                        G = gts[tb]
                        kb.tt(G, G[:, 0:8], pb[:, 0:8], parT[:, PART_OFF["bi"]:PART_OFF["bi"] + 8], ALU.add, [pb, parT])
                        kb.act(G, G[:, 0:8], G[:, 0:8], AF.Tanh, [G], scale=1.0 / 15.0)
                        kb.ts(G, G[:, 0:8], G[:, 0:8], 15.0, ALU.mult, [G])
                        kb.tt(G, G[:, 8:16], pb[:, 8:16], parT[:, PART_OFF["bf"]:PART_OFF["bf"] + 8], ALU.add, [pb, parT])
                        kb.act(G, G[:, 8:16], G[:, 8:16], AF.Tanh, [G], scale=1.0 / 15.0)
                        kb.act(G, G[:, 8:16], G[:, 8:16], AF.Sigmoid, [G], scale=15.0)
                        kb.act(G, G[:, 8:16], G[:, 8:16], AF.Ln, [G])
                        pf = pbank()
                        kb.mm(pf, pf[:, 0:8], m_iu, G[:, 8:16], [cstF, G])
                        kb.mm(pf, pf[:, 8:16], ones, G[:, 8:16], [cstF, G])
                        kb.cp(G, G[:, 16:24], pf[:, 0:8], [pf], eng="dve")
                        kb.act(G, G[:, 24:32], pf[:, 0:8], AF.Exp, [pf])
                        kb.act(G, G[:, 40:48], pf[:, 8:16], AF.Exp, [pf])
                        kb.tt(G, G[:, 32:40], G[:, 0:8], G[:, 16:24], ALU.subtract, [G])
                        kb.act(G, G[:, 32:40], G[:, 32:40], AF.Exp, [G])
            def head_gen(tb, hd):
                cs = slice(tb * L, (tb + 1) * L)
                G = gts[tb]
                pr_, hh = hd // 2, hd % 2
                hs = slice(hh * 64, (hh + 1) * 64)
                ix = hd % NHF
                es = slice(hd * 128, (hd + 1) * 128)
                C = Cst[pr_]
                kT = kTm[pr_]
                vg = Vg[ix]
                kb.ts(vg, vg[:, 0:128], vTb[tb][:, es], G[:, 32 + hd:33 + hd], ALU.mult, [vTb[tb], G])
                kb.cp(vg, vg[:, 128:129], G[:, 32 + hd:33 + hd], [G], eng="pool")
                pst = banks[2 * ix]
                kb.mm(pst, pst[:, 0:128], qkb[4 + pr_][hs, cs], qkb[pr_][hs, cs], [qkb[4 + pr_], qkb[pr_]])
                yield
                ST = STm[ix]
                kb.tt(ST, ST[:, :], pst[:, 0:128], m_iu, ALU.mult, [pst, cstF])
                pa_ = banks[2 * ix + 1]
                kb.mm(pa_, pa_[:, 0:129], qkb[pr_][hs, cs], C[hs, :], [qkb[pr_], C], start=True, stop=False)
                kb.mm(pa_, pa_[:, 0:129], ST[:, :], vg[:, 0:129], [ST, vg], start=False, stop=True)
                pc_ = banks[2 * ix]
                kb.mm(pc_, pc_[hs, 256:385], kT[:, hh * 64:(hh + 1) * 64], vg[:, 0:129], [kT, vg])
                yield
                ob = osb[ix]
                kb.ts(ob, ob[:, 0:129], pa_[:, 0:129], G[:, 24 + hd:25 + hd], ALU.mult, [pa_, G])
                ct = ctmp[ix]
                kb.tt(ct, ct[hs, 0:129], pc_[hs, 256:385], C[hs, :], ALU.add, [pc_, C])
                kb.ts(C, C[hs, :], ct[hs, 0:129], G[hs, 40 + hd:41 + hd], ALU.mult, [ct, G])
                yield
                kb.act(ob, ob[:, 129:130], ob[:, 128:129], AF.Abs, [ob])
                yield
                kb.ts(ob, ob[:, 129:130], ob[:, 129:130], 1.0, ALU.max, [ob])
                kb.op("dve", lambda e: e.reciprocal(out=ob[:, 130:131], in_=ob[:, 129:130]), reads=[ob], writes=[ob])
                yield
                hb_ = hhb[ix]
                kb.act(hb_, hb_[:, :], ob[:, 0:128], AF.Square, [ob], scale=ob[:, 130:131], accum_out=ob[:, 131:132])
                yield
                kb.act(ob, ob[:, 132:133], ob[:, 131:132], AF.Ln, [ob, cstF], scale=1.0 / 128.0, bias=eps_c)
                yield
                kb.act(ob, ob[:, 133:134], ob[:, 132:133], AF.Exp, [ob], scale=-0.5)
                kb.tt(ob, ob[:, 134:135], ob[:, 133:134], ob[:, 130:131], ALU.mult, [ob])
                kb.stt(hb_, hb_[:, :], ob[:, 0:128], ob[:, 134:135],
                       parT[:, PART_OFF["hnw"] + hd * 128:PART_OFF["hnw"] + (hd + 1) * 128],
                       ALU.mult, ALU.mult, [hb_, ob, parT])
                yield
                kb.tt(hgTb[tb], hgTb[tb][:, es], hb_[:, :], oTb[tb][:, es], ALU.mult, [hb_, oTb[tb]], eng="pool")
                yield

            def fm_gen(tb):
                cs = slice(tb * L, (tb + 1) * L)
                for kc in range(KC):
                    pt = (banks[6], ypsum)[kc % 2]
                    kb.tr(pt, pt[:, 0:128], hgTb[tb][:, kc * 128:(kc + 1) * 128], ident, [hgTb[tb], cstF])
                    kb.cp(ygb[kc], ygb[kc][:, cs], pt[:, 0:128], [pt])
                    yield

            def rr(gens):
                gens = list(gens)
                while gens:
                    for g_ in list(gens):
                        try:
                            next(g_)
                        except StopIteration:
                            gens.remove(g_)

            pend = []
            for tb in range(NCH):
                cs = slice(tb * L, (tb + 1) * L)
                for pr_ in range(4):
                    ptk = (banks[6], ypsum)[pr_ % 2]
                    kb.tr(ptk, ptk[:, 0:128], qkb[4 + pr_][:, cs], ident, [qkb[4 + pr_], cstF])
                    kb.cp(kTm[pr_], kTm[pr_][:, :], ptk[:, 0:128], [ptk])
                for g0 in range(0, 8, NHF):
                    gens = [head_gen(tb, hd) for hd in range(g0, min(8, g0 + NHF))]
                    if g0 == 0 and pend:
                        gens += pend
                        pend = []
                    rr(gens)
                pend = [fm_gen(tb)]
            rr(pend)
            outproj(1, ml_w_out, ygb)

        aoff[0] = 0
        ob_t = aalloc([P, KC, TT])
        obuf = Buf("obuf", ob_t)
        xT_v = xT.rearrange("(k p) t -> p k t", p=P)
        oT_v = outT.rearrange("(k p) t -> p k t", p=P)
        assert n_tiles % 2 == 0
        for tp in range(n_tiles // 2):
            for s_i in range(2):
                XO[0] = s_i * TT
                t0 = (2 * tp + s_i) * TT
                kb.dma(x_t[:, :, XO[0]:XO[0] + TT], xT_v[:, :, t0:t0 + TT], writes=xb)
                if "l0m" in stages:
                    l0_mixer(2 * tp + s_i)
            if "l0f" in stages:
                ffn(0)
            if "l1m" in stages:
                for s_i in range(2):
                    XO[0] = s_i * TT
                    l1_mixer(2 * tp + s_i)
            if "l1f" in stages:
                ffn(1)
            for s_i in range(2):
                XO[0] = s_i * TT
                t0 = (2 * tp + s_i) * TT
                if "fin" in stages:
                    kb.barrier()
                    ssb = pbank()
                    for c in range(KC):
                        s_ = sq[c % 2]
                        kb.act(s_, s_[:, :], xb[c][:, XO[0]:XO[0] + TT], AF.Square, [xb[c]])
                        kb.mm(ssb, ssb[:, 0:TT], onesD, s_[:, :], [s_, cstR], start=(c == 0), stop=(c == KC - 1))
                    kb.act(rs, rs[:, :], ssb[:, 0:TT], AF.Ln, [ssb, cstF], bias=eps_c)
                    kb.act(rs, rs[:, :], rs[:, :], AF.Exp, [rs], scale=-0.5)
                    for c in range(KC):
                        kb.stt(obuf, ob_t[:, c, :], xb[c][:, XO[0]:XO[0] + TT], pc("fw", c), rs[:, :],
                               ALU.mult, ALU.mult, [xb[c], rs, par])
                    kb.dma(oT_v[:, :, t0:t0 + TT], ob_t, reads=[obuf])
                else:
                    kb.dma(oT_v[:, :, t0:t0 + TT], x_t[:, :, XO[0]:XO[0] + TT], reads=xb)
        kb.finish()
        print("instructions", kb.ninst, "waits", kb.nwait, "arena", aoff[1], aoffR[0])
    return nc


TT_DEFAULT = 256


def make_in_maps(inp, TT, batches):
    cstF, cstR = make_consts(TT)
    parT = pack_part(inp)
    f = lambda a: np.ascontiguousarray(np.asarray(a, np.float32))
    shared = {
        "parT": parT, "cstF": cstF, "cstR": cstR,
        "ada_w": f(inp["ada_w"]), "rw_w_in": f(inp["rw_w_in"][0]), "rw_w2": f(inp["rw_w2"][0]),
        "rw_a2": f(inp["rw_a2"][0]), "rw_g2": f(inp["rw_g2"][0]), "rw_w_out": f(inp["rw_w_out"][0]),
        "ml_w_in": f(inp["ml_w_in"][0]), "ml_w_out": f(inp["ml_w_out"][0]),
        "ffn_w_gu": f(inp["ffn_w_gu"]), "ffn_w_down": f(inp["ffn_w_down"]),
    }
    maps = []
    for b in batches:
        m = dict(shared)
        m["xT"] = np.ascontiguousarray(np.asarray(inp["x"][b], np.float32).T)
        m["par"] = pack_par(inp, b)
        maps.append(m)
    return maps


def kernel(**inputs):
    TT = TT_DEFAULT
    nc = build_nc(TT=TT)
    batches = [0, 1, 2, 3, 0, 1, 2, 3]
    in_maps = make_in_maps(inputs, TT, batches)
    res = run_bass_kernel_spmd(nc, in_maps, core_ids=list(range(8)))
    out = np.stack([np.ascontiguousarray(res.results[b]["outT"].T) for b in range(4)], axis=0)
    return out.astype(np.float32)
```

```python
import contextlib
import os
import numpy as np
import concourse.bass as bass
import concourse.mybir as mybir
from concourse.bass_utils import run_bass_kernel_spmd

F32 = mybir.dt.float32
F32R = mybir.dt.float32r
BF16 = mybir.dt.bfloat16
AF = mybir.ActivationFunctionType
ALU = mybir.AluOpType

P = 128
D = 1024
KC = 8
S = 4096
DFF = 2816
RW_IN = 3360
ML_IN = 3088
EPS = 1e-6
GN_EPS = 64e-5
L = 128


class Buf:
    __slots__ = ("name", "ap", "w", "r")

    def __init__(self, name, ap):
        self.name = name
        self.ap = ap
        self.w = None
        self.r = {}

    def __getitem__(self, idx):
        return self.ap[idx]


class KB:
    def __init__(self, nc, stack, ndma=24):
        self.nc = nc
        self.stack = stack
        self.eng = {"pe": nc.tensor, "act": nc.scalar, "dve": nc.vector,
                    "pool": nc.gpsimd, "sp": nc.sync}
        self.sem = {}
        self.cnt = {}
        self.known = {e: {} for e in self.eng}
        for e in ["pe", "act", "dve", "pool"]:
            self.sem[e] = stack.enter_context(nc.semaphore("s_" + e))
            self.cnt[e] = 0
        self.ndma = ndma
        for i in range(ndma):
            self.sem[("d", i)] = stack.enter_context(nc.semaphore("d%d" % i))
            self.cnt[("d", i)] = 0
        self.dnext = 0
        self.hist = {}
        self.ninst = 0
        self.nwait = 0
        self.rr = 0

    def tile(self, name, shape, dtype=F32):
        return self.stack.enter_context(self.nc.sbuf_tensor("sb_" + name, list(shape), dtype))

    def sb(self, name, shape, dtype=F32):
        t = self.tile(name, shape, dtype)
        return Buf(name, t[:])

    def ps(self, name, shape, dtype=F32):
        t = self.stack.enter_context(self.nc.psum_tensor("ps_" + name, list(shape), dtype))
        return Buf(name, t[:])

    def _wait(self, e, deps):
        need = {}
        for k, v in deps:
            if v > need.get(k, 0):
                need[k] = v
        kn = self.known[e]
        for k, v in need.items():
            if e == "pe" and k == "pe":
                continue
            if kn.get(k, 0) >= v:
                continue
            self.eng[e].wait_ge(self.sem[k], v)
            kn[k] = v
            self.nwait += 1
            snap = self.hist.get((k, v))
            if snap:
                for k2, v2 in snap.items():
                    if kn.get(k2, 0) < v2:
                        kn[k2] = v2

    def _deps(self, reads, writes):
        deps = []
        for b in reads:
            if b.w:
                deps.append(b.w)
        for b in writes:
            if b.w:
                deps.append(b.w)
            deps.extend(b.r.items())
        return deps

    def _mark(self, tok, reads, writes):
        k, v = tok
        for b in reads:
            if b.r.get(k, 0) < v:
                b.r[k] = v
        for b in writes:
            b.w = tok
            b.r = {}

    def op(self, e, fn, reads=(), writes=()):
        self._wait(e, self._deps(reads, writes))
        inst = fn(self.eng[e])
        self.cnt[e] += 1
        inst.then_inc(self.sem[e], 1)
        tok = (e, self.cnt[e])
        self.hist[tok] = dict(self.known[e])
        self._mark(tok, reads, writes)
        self.ninst += 1
        return tok

    def dma(self, out, in_, reads=(), writes=(), q="sp"):
        i = self.dnext
        self.dnext = (self.dnext + 1) % self.ndma
        key = ("d", i)
        deps = self._deps(reads, writes)
        if self.cnt[key] > 0:
            deps.append((key, self.cnt[key]))
        self._wait(q, deps)
        inst = self.eng[q].dma_start(out=out, in_=in_)
        self.cnt[key] += 16
        inst.then_inc(self.sem[key], 16)
        tok = (key, self.cnt[key])
        self.hist[tok] = dict(self.known[q])
        self._mark(tok, reads, writes)
        self.ninst += 1
        return tok

    def barrier(self):
        toks = [(("d", i), self.cnt[("d", i)]) for i in range(self.ndma) if self.cnt[("d", i)] > 0]
        for k in ["pe", "act", "dve", "pool"]:
            if self.cnt[k] > 0:
                toks.append((k, self.cnt[k]))
        for e in ["pe", "act", "dve", "pool", "sp"]:
            self._wait(e, [t for t in toks if t[0] != e])

    def finish(self, e="sp"):
        toks = [(("d", i), self.cnt[("d", i)]) for i in range(self.ndma) if self.cnt[("d", i)] > 0]
        for k in ["pe", "act", "dve", "pool"]:
            if self.cnt[k] > 0:
                toks.append((k, self.cnt[k]))
        self._wait(e, toks)

    def mm(self, ob, out, lhsT, rhs, reads, start=True, stop=True):
        return self.op("pe", lambda e: e.matmul(out, lhsT, rhs, start=start, stop=stop),
                       reads=reads, writes=[ob])

    def tr(self, ob, out, in_, ident, reads):
        return self.op("pe", lambda e: e.transpose(out, in_, ident), reads=reads, writes=[ob])

    def tt(self, ob, out, in0, in1, op, reads, eng="dve"):
        return self.op(eng, lambda e: e.tensor_tensor(out=out, in0=in0, in1=in1, op=op),
                       reads=reads, writes=[ob])

    def ts(self, ob, out, in0, s1, op0, reads, s2=None, op1=None, eng="dve"):
        if op1 is None:
            return self.op(eng, lambda e: e.tensor_scalar(out=out, in0=in0, scalar1=s1, scalar2=None, op0=op0),
                           reads=reads, writes=[ob])
        return self.op(eng, lambda e: e.tensor_scalar(out=out, in0=in0, scalar1=s1, scalar2=s2, op0=op0, op1=op1),
                       reads=reads, writes=[ob])

    def stt(self, ob, out, in0, scalar, in1, op0, op1, reads):
        return self.op("dve", lambda e: e.scalar_tensor_tensor(out=out, in0=in0, scalar=scalar, in1=in1,
                                                                 op0=op0, op1=op1),
                       reads=reads, writes=[ob])

    def act(self, ob, out, in_, func, reads, bias=None, scale=1.0, accum_out=None):
        kw = {}
        if bias is not None:
            kw["bias"] = bias
        if accum_out is not None:
            kw["accum_out"] = accum_out
        return self.op("act", lambda e: e.activation(out=out, in_=in_, func=func, scale=scale, **kw),
                       reads=reads, writes=[ob])

    def cp(self, ob, out, in_, reads, eng=None):
        if eng is None:
            eng = ("act", "dve")[self.rr % 2]
            self.rr += 1
        if eng == "act":
            return self.op("act", lambda e: e.copy(out, in_), reads=reads, writes=[ob])
        return self.op(eng, lambda e: e.tensor_copy(out=out, in_=in_), reads=reads, writes=[ob])


def _fm(v):
    v = np.asarray(v, np.float32).reshape(-1)
    return np.ascontiguousarray(v.reshape(-1, 128).T)


PAR_LAYOUT = [("c", 8), ("adab0", 48), ("adab1", 48), ("mix", 48), ("w0", 8), ("a0", 8), ("kk", 8),
              ("ka", 8), ("rk", 8), ("gnw", 8), ("gnb", 8), ("cw", 32), ("cb", 8), ("fw", 8)]
PAR_OFF = {}
_o = 0
for _n, _w in PAR_LAYOUT:
    PAR_OFF[_n] = _o
    _o += _w
NPAR = _o

PART_LAYOUT = [("hnw", 1024), ("bi", 8), ("bf", 8)]
PART_OFF = {}
_o = 0
for _n, _w in PART_LAYOUT:
    PART_OFF[_n] = _o
    _o += _w
NPART = _o


def pack_par(inp, b):
    cols = [
        _fm(inp["c"][b]), _fm(inp["ada_b"][0]), _fm(inp["ada_b"][1]),
        np.concatenate([_fm(inp["rw_mix"][0, i]) for i in range(6)], axis=1),
        _fm(inp["rw_w0"][0]), _fm(inp["rw_a0"][0]), _fm(inp["rw_k_k"][0]), _fm(inp["rw_k_a"][0]),
        _fm(inp["rw_r_k"][0]), _fm(inp["rw_gn_w"][0]), _fm(inp["rw_gn_b"][0]),
        np.concatenate([_fm(inp["ml_conv_w"][0, j]) for j in range(4)], axis=1),
        _fm(inp["ml_conv_b"][0]), _fm(inp["final_w"]),
    ]
    out = np.concatenate(cols, axis=1).astype(np.float32)
    assert out.shape == (128, NPAR)
    return np.ascontiguousarray(out)


def pack_part(inp):
    row = np.concatenate([np.asarray(inp["ml_hn_w"][0], np.float32).reshape(-1),
                          np.asarray(inp["ml_b_i"][0], np.float32).reshape(-1),
                          np.asarray(inp["ml_b_f"][0], np.float32).reshape(-1)])
    return np.ascontiguousarray(np.broadcast_to(row[None, :], (128, NPART))).astype(np.float32)


def make_consts(TT):
    i = np.arange(128)
    ident = np.eye(128, dtype=np.float32)
    m_su = (i[:, None] < i[None, :]).astype(np.float32)
    m_sl = (i[:, None] > i[None, :]).astype(np.float32)
    m_iu = (i[:, None] <= i[None, :]).astype(np.float32)
    ones = np.ones((128, 128), np.float32)
    seg = np.ones((128, TT), np.float32)
    seg[:, ::L] = 0.0
    eps = np.zeros((128, 8), np.float32)
    eps[:, 0] = EPS
    eps[:, 1] = GN_EPS
    rep4 = lambda a: np.concatenate([a] * 4, axis=1)
    cstF = np.concatenate([ident, m_su, m_sl, m_iu, ones, seg, eps, rep4(ident), rep4(m_su), rep4(m_sl), rep4(m_iu)],
                          axis=1)
    blk = (i[:, None] // 64 == i[None, :] // 64).astype(np.float32)
    cstR = np.concatenate([blk / 64.0, blk, ones / 1024.0], axis=1)
    return np.ascontiguousarray(cstF), np.ascontiguousarray(cstR)


def build_nc(TT=256, n_tiles=None, stages=("l0m", "l0f", "l1m", "l1f", "fin"), NSLOT=4):
    ARENA_WORDS = 12288 * TT // 256
    NCH = TT // L
    DBG = os.environ.get('KDBG', '').split(',')
    if n_tiles is None:
        n_tiles = S // TT
    nc = bass.Bass("TRN2", target_bir_lowering=False)
    nc.dge_precook = False
    dram = lambda n, sh, k="ExternalInput": nc.dram_tensor(n, list(sh), F32, kind=k).ap()
    xT = dram("xT", [D, S])
    par_d = dram("par", [P, NPAR])
    part_d = dram("parT", [P, NPART])
    NCF = 5 * 128 + TT + 8 + 4 * 512
    cstF_d = dram("cstF", [P, NCF])
    cstR_d = dram("cstR", [P, 3 * 128])
    ada_w = dram("ada_w", [2, D, 6 * D])
    rw_w_in = dram("rw_w_in", [D, RW_IN])
    rw_w2 = dram("rw_w2", [64, D])
    rw_a2 = dram("rw_a2", [64, D])
    rw_g2 = dram("rw_g2", [160, D])
    rw_w_out = dram("rw_w_out", [D, D])
    ml_w_in = dram("ml_w_in", [D, ML_IN])
    ml_w_out = dram("ml_w_out", [D, D])
    w_gu = dram("ffn_w_gu", [2, D, 2 * DFF])
    w_dn = dram("ffn_w_down", [2, DFF, D])
    outT = dram("outT", [D, S], "ExternalOutput")

    with contextlib.ExitStack() as st:
        kb = KB(nc, st)
        cstF = kb.sb("cstF", [P, NCF])
        cstR = kb.sb("cstR", [P, 3 * 128], F32R)
        par = kb.sb("par", [P, NPAR])
        parT = kb.sb("parT", [P, NPART])
        kb.dma(cstF[:], cstF_d, writes=[cstF])
        kb.dma(cstR[:], cstR_d.bitcast(F32R), writes=[cstR])
        kb.dma(par[:], par_d, writes=[par])
        kb.dma(parT[:], part_d, writes=[parT])
        ident = cstF[:, 0:128]
        m_su = cstF[:, 128:256]
        m_sl = cstF[:, 256:384]
        m_iu = cstF[:, 384:512]
        ones = cstF[:, 512:640]
        segm = cstF[:, 640:640 + TT]
        eps_c = cstF[:, 640 + TT:640 + TT + 1]
        gneps_c = cstF[:, 640 + TT + 1:640 + TT + 2]
        c4 = 640 + TT + 8
        ident4 = cstF[:, c4:c4 + 512]
        m_su4 = cstF[:, c4 + 512:c4 + 1024]
        m_sl4 = cstF[:, c4 + 1024:c4 + 1536]
        m_iu4 = cstF[:, c4 + 1536:c4 + 2048]
        bones64 = cstR[:, 0:128]
        bones1 = cstR[:, 128:256]
        onesD = cstR[:, 256:384]
        pc = lambda n, j=0, w=1: par[:, PAR_OFF[n] + j:PAR_OFF[n] + j + w]

        der = kb.sb("der", [P, 2 * 48 + 2 * 16 + 8 + 8])
        modc = lambda i, j, w=1: der[:, i * 48 + j:i * 48 + j + w]
        sc1p = lambda i, c: der[:, 96 + i * 16 + c:96 + i * 16 + c + 1]
        sc2p = lambda i, c: der[:, 96 + i * 16 + 8 + c:96 + i * 16 + 8 + c + 1]
        oka = lambda m: der[:, 128 + m:128 + m + 1]
        cact = lambda kc: der[:, 136 + kc:136 + kc + 1]

        SLOTW = 2048
        slots_t = kb.tile("wslots", [P, NSLOT * SLOTW], F32R)
        slots = [Buf("ws%d" % i, slots_t[:, i * SLOTW:(i + 1) * SLOTW]) for i in range(NSLOT)]
        slot_i = [0]

        def wslot():
            s_ = slots[slot_i[0] % NSLOT]
            slot_i[0] += 1
            return s_

        def wslot2():
            if slot_i[0] % 2:
                slot_i[0] += 1
            i0 = slot_i[0] % NSLOT
            slot_i[0] += 2
            return [slots[i0], slots[i0 + 1]], slots_t[:, i0 * SLOTW:(i0 + 2) * SLOTW]

        NROT = 7
        banks = [kb.ps("pb%d" % i, [P, 512]) for i in range(NROT)]
        ypsum = kb.ps("ypsum", [P, 512])
        bank_i = [0]

        def pbank():
            b_ = banks[bank_i[0] % NROT]
            bank_i[0] += 1
            return b_

        ARW = ARENA_WORDS
        arena_t = kb.tile("arena", [P, ARW])
        ARWR = 6144 * TT // 256
        arenaR_t = kb.tile("arenaR", [P, ARWR], F32R)
        aoff = [0, 0]
        aoffR = [0]

        def aalloc(shape, dt=F32):
            words = 1
            for d_ in shape[1:]:
                words *= d_
            words += words % 2
            if dt == F32R:
                ap = arenaR_t[0:shape[0], aoffR[0]:aoffR[0] + words]
                aoffR[0] += words
                assert aoffR[0] <= ARWR, (aoffR[0], ARWR)
            else:
                ap = arena_t[0:shape[0], aoff[0]:aoff[0] + words]
                aoff[0] += words
                aoff[1] = max(aoff[1], aoff[0])
                assert aoff[0] <= ARW, (aoff[0], ARW)
            if len(shape) == 3:
                ap = ap.rearrange("p (a b) -> p a b", a=shape[1])
            return ap

        def asb(name, shape, dt=F32):
            return Buf(name, aalloc(shape, dt))

        x_t = kb.tile("x", [P, KC, 2 * TT])
        xb = [Buf("x%d" % c, x_t[:, c, :]) for c in range(KC)]
        XO = [0]
        h_t = kb.tile("h", [P, KC, TT + 1], F32R)
        hb = [Buf("h%d" % c, h_t[:, c, :]) for c in range(KC)]
        yg_t = kb.tile("yg", [P, KC, TT], F32R)
        ygb = [Buf("yg%d" % c, yg_t[:, c, :]) for c in range(KC)]
        xs_all = kb.tile("xs", [P, KC, 3 * TT], F32R)
        xsb = [[Buf("xs%d_%d" % (i, c), xs_all[:, c, i * TT:(i + 1) * TT]) for c in range(KC)] for i in range(3)]
        h2b = [Buf("h2_%d" % c, xs_all[:, c, 0:2 * TT]) for c in range(KC)]
        sq = [Buf("sq%d" % i, None) for i in range(2)]
        rs = kb.sb("rs", [P, TT])
        tmpn = [kb.sb("tmpn%d" % i, [P, TT]) for i in range(2)]

        for c in range(KC):
            kb.ts(hb[c], hb[c][:, 0:1], cstF[:, 0:1], 0.0, ALU.mult, [cstF])

        kb.act(der, der[:, 136:144], pc("c", 0, 8), AF.Silu, [par])
        for kc in range(KC):
            kb.ts(h2b[kc], h2b[kc][:, 0:128], cstF[:, 0:128], 0.0, ALU.mult, [cstF])
            kb.cp(h2b[kc], h2b[kc][:, 0:1], cact(kc), [der], eng="act")
        rowb = asb("rowb", [1, 512])
        for i in range(2):
            mp = ypsum
            for blk in range(12):
                wbufs, wap = wslot2()
                wvr = wap.rearrange("p (k n) -> p k n", k=KC)
                kb.dma(wvr, ada_w[i].rearrange("(k p) n -> p k n", p=P)[:, :, blk * 512:(blk + 1) * 512].bitcast(F32R),
                       writes=wbufs)
                pr_ = pbank()
                for kc in range(KC):
                    kb.mm(pr_, pr_[:, 0:512], h2b[kc][:, 0:128], wvr[:, kc, :], wbufs + [h2b[kc]],
                          start=(kc == 0), stop=(kc == KC - 1))
                kb.cp(rowb, rowb[0:1, :], pr_[0:1, 0:512], [pr_], eng="act")
                for jj in range(4):
                    j = blk * 4 + jj
                    kb.mm(mp, mp[:, j:j + 1], rowb[0:1, jj * 128:(jj + 1) * 128], ones[0:1, 0:1], [rowb, cstF])
            kb.tt(der, der[:, i * 48:(i + 1) * 48], mp[:, 0:48], pc("adab%d" % i, 0, 48), ALU.add, [mp, par])
            kb.ts(der, der[:, 96 + i * 16:96 + i * 16 + 8], modc(i, 8, 8), 1.0, ALU.add, [der])
            kb.ts(der, der[:, 96 + i * 16 + 8:96 + i * 16 + 16], modc(i, 32, 8), 1.0, ALU.add, [der])
        kb.ts(der, der[:, 128:136], pc("ka", 0, 8), -1.0, ALU.mult, [par], s2=1.0, op1=ALU.add)

        Hst = [kb.sb("H%d" % m, [P, 128]) for m in range(8)]
        Cst = [kb.sb("C%d" % m, [P, 129]) for m in range(4)]
        for b_ in Hst + Cst:
            kb.op("pool", lambda e: e.memset(b_[:, :], 0.0), writes=[b_])
        qkpre_t = kb.tile("qkpre", [P, KC, TT + 3])
        qkpre = [Buf("qkpre%d" % c, qkpre_t[:, c, :]) for c in range(KC)]
        kb.op("pool", lambda e: e.memset(qkpre_t[:, :, 0:3], 0.0), writes=qkpre)

        lw2 = [asb("lw2_%d" % i, [64, 128], F32R) for i in range(2)]
        la2 = [asb("la2_%d" % i, [64, 128], F32R) for i in range(2)]
        lg2 = [asb("lg2_%d" % i, [P, 2, 128], F32R) for i in range(2)]

        def load_lora(m):
            msl_ = slice(m * 128, (m + 1) * 128)
            kb.dma(lw2[m % 2][:, :], rw_w2[:, msl_].bitcast(F32R), writes=[lw2[m % 2]])
            kb.dma(la2[m % 2][:, :], rw_a2[:, msl_].bitcast(F32R), writes=[la2[m % 2]])
            kb.dma(lg2[m % 2][0:96, 0, :], rw_g2[0:96, msl_].bitcast(F32R), writes=[lg2[m % 2]])
            kb.dma(lg2[m % 2][64:128, 1, :], rw_g2[96:160, msl_].bitcast(F32R), writes=[lg2[m % 2]])

        def wk(name, n=1, dt=F32, shape=None):
            return [asb("%s%d" % (name, i), shape or [P, TT], dt) for i in range(n)]

        def normmod(i, scp, shoff, outb, outoff):
            ssb = pbank()
            for c in range(KC):
                s_ = sq[c % 2]
                kb.act(s_, s_[:, :], xb[c][:, XO[0]:XO[0] + TT], AF.Square, [xb[c]])
                kb.mm(ssb, ssb[:, 0:TT], onesD, s_[:, :], [s_, cstR], start=(c == 0), stop=(c == KC - 1))
            kb.act(rs, rs[:, :], ssb[:, 0:TT], AF.Ln, [ssb, cstF], bias=eps_c)
            kb.act(rs, rs[:, :], rs[:, :], AF.Exp, [rs], scale=-0.5)
            for c in range(KC):
                t_ = tmpn[c % 2]
                kb.stt(t_, t_[:, :], xb[c][:, XO[0]:XO[0] + TT], scp(i, c), rs[:, :], ALU.mult, ALU.mult, [xb[c], rs, der])
                kb.act(outb[c], outb[c][:, outoff:outoff + TT], t_[:, :], AF.Identity, [t_, der],
                       bias=modc(i, shoff + c))

        def outproj(i, w_d, srcb):
            for op_ in range(KC // 2):
                ws = wslot()
                wv = ws[:, 0:KC * 256].rearrange("p (k n) -> p k n", k=KC)
                kb.dma(wv, w_d.rearrange("(k p) n -> p k n", p=P)[:, :, op_ * 256:(op_ + 1) * 256].bitcast(F32R),
                       writes=[ws])
                for oo in range(2):
                    o = op_ * 2 + oo
                    pb = pbank()
                    for kc in range(KC):
                        kb.mm(pb, pb[:, 0:TT], wv[:, kc, oo * 128:(oo + 1) * 128], srcb[kc][:, 0:TT], [ws, srcb[kc]],
                              start=(kc == 0), stop=(kc == KC - 1))
                    kb.stt(xb[o], xb[o][:, XO[0]:XO[0] + TT], pb[:, 0:TT], modc(i, 16 + o),
                           xb[o][:, XO[0]:XO[0] + TT], ALU.mult, ALU.add, [pb, xb[o], der])

        TP = 2 * TT
        for i_ in range(2):
            sq[i_].ap = aalloc([P, TT], F32R)
        hff_t = aalloc([P, 4, TP], F32R)
        hff = [Buf("hff%d" % j, hff_t[:, j, :]) for j in range(4)]
        sgb = [asb("sgb%d" % i, [P, TP]) for i in range(2)]
        aoff[0] = 0

        def ffn(i):
            kb.barrier()
            for s_i in range(2):
                XO[0] = s_i * TT
                normmod(i, sc2p, 24, h2b, s_i * TT)
            ngroups = (DFF // 128 + 3) // 4
            for g in range(ngroups):
                nj = min(4, DFF // 128 - g * 4)
                src = w_gu[i].rearrange("(k p) n -> p k n", p=P)
                for jp in range(nj // 2):
                    f = g * 4 + jp * 2
                    wsA = wslot()
                    wvA = wsA[:, 0:KC * 256].rearrange("p (k n) -> p k n", k=KC)
                    kb.dma(wvA, src[:, :, f * 128:(f + 2) * 128].bitcast(F32R), writes=[wsA])
                    wsB = wslot()
                    wvB = wsB[:, 0:KC * 256].rearrange("p (k n) -> p k n", k=KC)
                    kb.dma(wvB, src[:, :, DFF + f * 128:DFF + (f + 2) * 128].bitcast(F32R), writes=[wsB])
                    for jj in range(2):
                        j = jp * 2 + jj
                        csl = slice(jj * 128, (jj + 1) * 128)
                        pg = pbank()
                        for kc in range(KC):
                            kb.mm(pg, pg[:, 0:TP], wvA[:, kc, csl], h2b[kc][:, :], [wsA, h2b[kc]],
                                  start=(kc == 0), stop=(kc == KC - 1))
                        pu = pbank()
                        for kc in range(KC):
                            kb.mm(pu, pu[:, 0:TP], wvB[:, kc, csl], h2b[kc][:, :], [wsB, h2b[kc]],
                                  start=(kc == 0), stop=(kc == KC - 1))
                        s_ = sgb[j % 2]
                        kb.act(s_, s_[:, :], pg[:, 0:TP], AF.Silu, [pg])
                        kb.tt(hff[j], hff[j][:, :], s_[:, :], pu[:, 0:TP], ALU.mult, [s_, pu])
                for half in range(2):
                    ws = wslot()
                    wv = ws[:, 0:nj * 512].rearrange("p (k n) -> p k n", k=nj)
                    kb.dma(wv, w_dn[i][g * 512:g * 512 + nj * 128, :].rearrange("(k p) n -> p k n", p=P)[
                        :, :, half * 512:(half + 1) * 512].bitcast(F32R), writes=[ws])
                    for oo in range(4):
                        o = half * 4 + oo
                        pb = pbank()
                        for j in range(nj):
                            kb.mm(pb, pb[:, 0:TP], wv[:, j, oo * 128:(oo + 1) * 128], hff[j][:, :],
                                  [ws, hff[j]], start=(j == 0), stop=(j == nj - 1))
                        kb.stt(xb[o], xb[o][:, :], pb[:, 0:TP], modc(i, 40 + o), xb[o][:, :], ALU.mult, ALU.add,
                               [pb, xb[o], der])

        wl = asb("wl", [64, TT], F32R)
        al = asb("al", [64, TT], F32R)
        gl = asb("gl", [P, 2, TT], F32R)

        def wk(name, n=1, dt=F32, shape=None):
            return [asb("%s%d" % (name, i), shape or [P, TT], dt) for i in range(n)]

        r_b, k0_b, k_b = wk("r")[0], wk("k0")[0], wk("k")[0]
        lw_b, cl_b, asg_b = wk("lw")[0], wk("cl")[0], wk("asg")[0]
        kkn_b, b_b = wk("kkn")[0], wk("bb")[0]
        kksq_b = wk("kksq", 1, F32R)[0]
        e2_b, e3_b = wk("e2")[0], wk("e3")[0]
        t1_b, t2_b = wk("t1")[0], wk("t2")[0]
        rkp_b = wk("rkp", 1, F32R)[0]
        DB = []
        for i_ in range(2):
            d_ = {}
            for nm in ("At", "Rt", "v", "Bp", "Kp", "e1", "bon", "g"):
                d_[nm] = asb("%s_%d" % (nm, i_), [P, TT])
            for nm in ("Atb", "Btb", "Ktb", "Rtb"):
                d_[nm] = kb.sb("%s_%d" % (nm, i_), [P, TT], BF16)
            d_["ys"] = asb("ys_%d" % i_, [P, TT], F32R)
            DB.append(d_)
        yc_b = wk("yc")[0]
        ysq_b = wk("ysq", 1, F32R)[0]
        t1c_b = wk("t1c")[0]
        sm = lambda name, n, w=128: [asb("%s%d" % (name, i), [P, w]) for i in range(n)]
        NCHAIN = 4
        big = lambda name, w=NCHAIN * 128: asb(name, [P, w])
        S1 = [big("S1_%d" % i) for i in range(2)]
        S2 = [big("S2_%d" % i) for i in range(2)]
        bfb = lambda name, w=NCHAIN * 128: kb.sb(name, [P, w], BF16)
        S3 = [bfb("S3_%d" % i, 256) for i in range(2)]
        P0b = [bfb("P0b_%d" % i, 256) for i in range(2)]
        PmA, PTmA, PmB, PTmB = bfb("PmA"), bfb("PTmA"), bfb("PmB"), bfb("PTmB")
        QmA = big("QmA")
        Qbf = bfb("Qbf")
        TM3 = [asb("TM3_%d" % i, [P, 3 * 128]) for i in range(2)]
        Zs, Us = sm("Zs", 2, 128), sm("Us", 2, 128)

        def inproj_fm(w_d, col0, n, rhs_list, pb, mix=None):
            ws = wslot()
            nk = len(rhs_list)
            wv = ws[:, 0:KC * 128].rearrange("p (k n) -> p k n", k=KC)
            kb.dma(wv[:, 0:KC, 0:n], w_d.rearrange("(k p) n -> p k n", p=P)[:, :, col0:col0 + n].bitcast(F32R),
                   writes=[ws])
            for kk_ in range(nk):
                rb, rap = rhs_list[kk_]
                kb.mm(pb, pb[0:n, 0:TT], wv[:, kk_, 0:n], rap, [ws, rb], start=(kk_ == 0), stop=(kk_ == nk - 1))

        def mkxs(mix, dst):
            for c in range(KC):
                kb.stt(dst[c], dst[c][:, 0:TT], qkpre[c][:, 3:3 + TT], pc("mix", mix * 8 + c),
                       hb[c][:, 1:TT + 1].bitcast(F32), ALU.mult, ALU.add, [qkpre[c], hb[c], par])
            return [(dst[c], dst[c][:, 0:TT]) for c in range(KC)]

        def stage_A(m, x0, x2, x3):
            D_ = DB[m % 2]
            At_b, Rt_b, v_b, Bp_b, Kp_b, e1_b, bon_b, g_b = (D_["At"], D_["Rt"], D_["v"], D_["Bp"], D_["Kp"],
                                                                   D_["e1"], D_["bon"], D_["g"])
            Atb, Btb, Ktb, Rtb = D_["Atb"], D_["Btb"], D_["Ktb"], D_["Rtb"]
            if m + 1 < 8:
                load_lora(m + 1)
            pr = pbank()
            inproj_fm(rw_w_in, m * 128, 128, x0, pr)
            kb.cp(r_b, r_b[:, :], pr[:, 0:TT], [pr], eng="act")
            yield
            pk = pbank()
            inproj_fm(rw_w_in, 1088 + m * 128, 128, x2, pk)
            kb.cp(k0_b, k0_b[:, :], pk[:, 0:TT], [pk], eng="dve")
            yield
            pv = pbank()
            inproj_fm(rw_w_in, 2112 + m * 128, 128, x3, pv)
            kb.cp(v_b, v_b[:, :], pv[:, 0:TT], [pv], eng="act")
            yield
            pw = pbank()
            kb.mm(pw, pw[:, 0:TT], lw2[m % 2][:, :], wl[:, :], [lw2[m % 2], wl])
            kb.act(lw_b, lw_b[:, :], pw[:, 0:TT], AF.Sigmoid, [pw, par], bias=pc("w0", m))
            pa = pbank()
            kb.mm(pa, pa[:, 0:TT], la2[m % 2][:, :], al[:, :], [la2[m % 2], al])
            kb.act(asg_b, asg_b[:, :], pa[:, 0:TT], AF.Sigmoid, [pa, par], bias=pc("a0", m))
            yield
            kb.ts(lw_b, lw_b[:, :], lw_b[:, :], -0.6065306597126334, ALU.mult, [lw_b], eng="dve")
            pg_ = pbank()
            kb.mm(pg_, pg_[:, 0:TT], lg2[m % 2][0:96, 0, :], gl[0:96, 0, :], [lg2[m % 2], gl], start=True, stop=False)
            kb.mm(pg_, pg_[:, 0:TT], lg2[m % 2][64:128, 1, :], gl[64:128, 1, :], [lg2[m % 2], gl], start=False, stop=True)
            kb.cp(g_b, g_b[:, :], pg_[:, 0:TT], [pg_], eng="act")
            yield
            kb.ts(kkn_b, kkn_b[:, :], k0_b[:, :], pc("kk", m), ALU.mult, [k0_b, par])
            kb.act(kksq_b, kksq_b[:, :], kkn_b[:, :], AF.Square, [kkn_b])
            pn = pbank()
            kb.mm(pn, pn[:, 0:TT], bones1, kksq_b[:, :], [cstR, kksq_b])
            kb.ts(t1_b, t1_b[:, :], pn[:, 0:TT], 1e-18, ALU.max, [pn])
            yield
            kb.op("dve", lambda e: e.tensor_tensor_scan(out=cl_b[:, :], data0=segm, data1=lw_b[:, :],
                                                          initial=0.0, op0=ALU.mult, op1=ALU.add),
                  reads=[cstF, lw_b], writes=[cl_b])
            yield
            kb.act(t1_b, t1_b[:, :], t1_b[:, :], AF.Ln, [t1_b])
            kb.act(t1_b, t1_b[:, :], t1_b[:, :], AF.Exp, [t1_b], scale=-0.5)
            yield
            kb.tt(kkn_b, kkn_b[:, :], kkn_b[:, :], t1_b[:, :], ALU.mult, [kkn_b, t1_b])
            kb.ts(t2_b, t2_b[:, :], asg_b[:, :], pc("ka", m), ALU.mult, [asg_b, par, der], s2=oka(m), op1=ALU.add)
            yield
            kb.tt(k_b, k_b[:, :], k0_b[:, :], t2_b[:, :], ALU.mult, [k0_b, t2_b])
            kb.tt(b_b, b_b[:, :], kkn_b[:, :], asg_b[:, :], ALU.mult, [kkn_b, asg_b], eng="pool")
            yield
            kb.act(e1_b, e1_b[:, :], cl_b[:, :], AF.Exp, [cl_b])
            kb.act(e2_b, e2_b[:, :], cl_b[:, :], AF.Exp, [cl_b], scale=-1.0)
            kb.tt(t1_b, t1_b[:, :], cl_b[:, :], lw_b[:, :], ALU.subtract, [cl_b, lw_b], eng="pool")
            yield
            kb.act(e3_b, e3_b[:, :], t1_b[:, :], AF.Exp, [t1_b])
            kb.tt(Btb, Btb[:, :], b_b[:, :], e2_b[:, :], ALU.mult, [b_b, e2_b])
            kb.tt(Ktb, Ktb[:, :], k_b[:, :], e2_b[:, :], ALU.mult, [k_b, e2_b], eng="pool")
            yield
            kb.tt(Rt_b, Rt_b[:, :], r_b[:, :], e1_b[:, :], ALU.mult, [r_b, e1_b])
            kb.cp(Rtb, Rtb[:, :], Rt_b[:, :], [Rt_b], eng="pool")
            kb.stt(At_b, At_b[:, :], kkn_b[:, :], -1.0, e3_b[:, :], ALU.mult, ALU.mult, [kkn_b, e3_b])
            kb.cp(Atb, Atb[:, :], At_b[:, :], [At_b], eng="pool")
            yield
            for c in range(NCH):
                cs = slice(c * L, (c + 1) * L)
                kb.act(e3_b, e3_b[:, cs], cl_b[:, cs], AF.Exp, [cl_b], scale=-1.0,
                       bias=cl_b[:, (c + 1) * L - 1:(c + 1) * L])
            yield
            kb.tt(Bp_b, Bp_b[:, :], b_b[:, :], e3_b[:, :], ALU.mult, [b_b, e3_b])
            kb.tt(Kp_b, Kp_b[:, :], k_b[:, :], e3_b[:, :], ALU.mult, [k_b, e3_b], eng="pool")
            yield
            kb.stt(rkp_b, rkp_b[:, :], r_b[:, :], pc("rk", m), k_b[:, :], ALU.mult, ALU.mult, [r_b, k_b, par])
            pbn = pbank()
            kb.mm(pbn, pbn[:, 0:TT], bones1, rkp_b[:, :], [cstR, rkp_b])
            kb.tt(bon_b, bon_b[:, :], pbn[:, 0:TT], v_b[:, :], ALU.mult, [pbn, v_b])
            yield

        def stage_B(m):
            D_ = DB[m % 2]
            At_b, Rt_b, v_b, Bp_b, Kp_b, e1_b = D_["At"], D_["Rt"], D_["v"], D_["Bp"], D_["Kp"], D_["e1"]
            Atb, Btb, Ktb, Rtb, ysb = D_["Atb"], D_["Btb"], D_["Ktb"], D_["Rtb"], D_["ys"]
            H = Hst[m]
            for g0 in range(0, NCH, 2):
                gchunks = list(range(g0, min(g0 + 2, NCH)))
                ncg = len(gchunks)
                for c in gchunks:
                    cs = slice(c * L, (c + 1) * L)
                    pt = pbank()
                    for j_, src in enumerate((v_b, Bp_b, Kp_b)):
                        kb.tr(pt, pt[:, j_ * 128:(j_ + 1) * 128], src[:, cs], ident, [src, cstF])
                    kb.cp(TM3[c % 2], TM3[c % 2][:, :], pt[:, 0:384], [pt])
                yield
                def score_bank(hh, pairs, dst, msk4):
                    hs = slice(hh * 64, (hh + 1) * 64)
                    pp = pbank()
                    W_ = len(pairs) * 256
                    for j_, (lb, rb_) in enumerate(pairs):
                        for ci, c in enumerate(gchunks):
                            cs = slice(c * L, (c + 1) * L)
                            o_ = j_ * 256 + ci * 128
                            kb.mm(pp, pp[:, o_:o_ + 128], lb[hs, cs], rb_[hs, cs], [lb, rb_])
                    if ncg == 2:
                        kb.tt(dst, dst[:, 0:W_], pp[:, 0:W_], msk4[:, 0:W_], ALU.mult, [pp, cstF])
                    else:
                        for j_ in range(len(pairs)):
                            kb.tt(dst, dst[:, j_ * 256:j_ * 256 + 128], pp[:, j_ * 256:j_ * 256 + 128],
                                  msk4[:, 0:128], ALU.mult, [pp, cstF])

                for hh in range(2):
                    score_bank(hh, ((Btb, Atb), (Ktb, Atb)), S1[hh], m_su4)
                    score_bank(hh, ((Atb, Btb),), S3[hh], m_sl4)
                    yield
                chains = [(hh, ci) for hh in range(2) for ci in range(ncg)]
                ks = lambda k_: slice(k_ * 128, (k_ + 1) * 128)
                kq = lambda hh, ci: hh * 2 + ci
                Pc = [(P0b[hh], slice(ci * 128, (ci + 1) * 128)) for hh, ci in chains]
                PTc = [(S3[hh], slice(ci * 128, (ci + 1) * 128)) for hh, ci in chains]
                for hh in range(2):
                    kb.tt(QmA, QmA[:, hh * 256:hh * 256 + ncg * 128], S1[hh][:, 0:ncg * 128],
                          ident4[:, 0:ncg * 128], ALU.add, [S1[hh], cstF], eng="pool")
                    kb.cp(P0b[hh], P0b[hh][:, 0:ncg * 128], S1[hh][:, 0:ncg * 128], [S1[hh]], eng="act")
                kb.cp(Qbf, Qbf[:, :], QmA[:, :], [QmA], eng="dve")
                yield
                for hh in range(2):
                    score_bank(hh, ((Btb, Rtb), (Ktb, Rtb)), S2[hh], m_iu4)
                yield
                for it in range(1, 7):
                    last = (it == 6)
                    b1 = pbank()
                    for (hh, ci), (pb_, psl), (ptb_, ptsl) in zip(chains, Pc, PTc):
                        kb.mm(b1, b1[:, ks(kq(hh, ci))], pb_[:, psl], ptb_[:, ptsl], [pb_, ptb_])
                    if not last:
                        b2 = pbank()
                        for (hh, ci), (pb_, psl), (ptb_, ptsl) in zip(chains, Pc, PTc):
                            kb.mm(b2, b2[:, ks(kq(hh, ci))], ptb_[:, ptsl], pb_[:, psl], [pb_, ptb_])
                    nPT = PTmA if (it % 2 == 1) else PTmB
                    kb.cp(nPT, nPT[:, :], b1[:, :], [b1], eng="act")
                    PTc = [(nPT, ks(kq(hh, ci))) for hh, ci in chains]
                    if not last:
                        nP = PmA if (it % 2 == 1) else PmB
                        kb.cp(nP, nP[:, :], b2[:, :], [b2], eng="act")
                        Pc = [(nP, ks(kq(hh, ci))) for hh, ci in chains]
                    yield
                    b3 = pbank()
                    for (hh, ci), (ptb_, ptsl) in zip(chains, PTc):
                        kb.mm(b3, b3[:, ks(kq(hh, ci))], ptb_[:, ptsl], Qbf[:, ks(kq(hh, ci))], [ptb_, Qbf])
                    if not last:
                        kb.tt(Qbf, Qbf[:, :], b3[:, :], QmA[:, :], ALU.add, [b3, QmA])
                    kb.tt(QmA, QmA[:, :], b3[:, :], QmA[:, :], ALU.add, [b3, QmA])
                    yield
                for ci, c in enumerate(gchunks):
                    cs = slice(c * L, (c + 1) * L)
                    tm = TM3[c % 2]
                    pz = pbank()
                    kb.mm(pz, pz[:, 0:128], At_b[:, cs], H[:, :], [At_b, H], start=True, stop=False)
                    for hh in range(2):
                        vs = slice(hh * 64, (hh + 1) * 64)
                        kb.mm(pz, pz[:, vs], S1[hh][:, 256 + ci * 128:256 + (ci + 1) * 128], tm[:, vs],
                              [S1[hh], tm], start=False, stop=(hh == 1))
                    zsb = Zs[c % 2]
                    kb.cp(zsb, zsb[:, :], pz[:, 0:128], [pz], eng="act")
                    yield
                    pu_ = pbank()
                    for hh in range(2):
                        vs = slice(hh * 64, (hh + 1) * 64)
                        kb.mm(pu_, pu_[:, vs], QmA[:, ks(kq(hh, ci))], zsb[:, vs], [QmA, zsb])
                    usb = Us[c % 2]
                    kb.cp(usb, usb[:, :], pu_[:, 0:128], [pu_], eng="dve")
                    yield
                    kb.mm(ypsum, ypsum[:, cs], H[:, :], Rt_b[:, cs], [H, Rt_b], start=True, stop=False)
                    for hh in range(2):
                        hs = slice(hh * 64, (hh + 1) * 64)
                        vs = slice(hh * 64, (hh + 1) * 64)
                        yo = ypsum[hs, cs]
                        kb.mm(ypsum, yo, usb[:, vs], S2[hh][:, ci * 128:(ci + 1) * 128], [usb, S2[hh]],
                              start=False, stop=False)
                        kb.mm(ypsum, yo, tm[:, vs], S2[hh][:, 256 + ci * 128:256 + (ci + 1) * 128], [tm, S2[hh]],
                              start=False, stop=True)
                    ph = pbank()
                    for hh in range(2):
                        hs = slice(hh * 64, (hh + 1) * 64)
                        vs = slice(hh * 64, (hh + 1) * 64)
                        kb.mm(ph, ph[hs, 0:64], tm[:, 128 + hh * 64:128 + (hh + 1) * 64], usb[:, vs], [tm, usb],
                              start=True, stop=False)
                        kb.mm(ph, ph[hs, 0:64], tm[:, 256 + hh * 64:256 + (hh + 1) * 64], tm[:, vs], [tm],
                              start=False, stop=True)
                    for hh in range(2):
                        hs = slice(hh * 64, (hh + 1) * 64)
                        vs = slice(hh * 64, (hh + 1) * 64)
                        kb.stt(H, H[hs, vs], H[hs, vs], e1_b[hs, (c + 1) * L - 1:(c + 1) * L], ph[hs, 0:64],
                               ALU.mult, ALU.add, [H, e1_b, ph])
                    yield
            kb.cp(ysb, ysb[:, :], ypsum[:, 0:TT], [ypsum], eng="act")
            yield

        def stage_C(m):
            D_ = DB[m % 2]
            ysb, bon_b, g_b = D_["ys"], D_["bon"], D_["g"]
            pm_ = pbank()
            kb.mm(pm_, pm_[:, 0:TT], bones64, ysb[:, :], [cstR, ysb])
            kb.tt(yc_b, yc_b[:, :], ysb[:, :].bitcast(F32), pm_[:, 0:TT], ALU.subtract, [ysb, pm_])
            yield
            kb.act(ysq_b, ysq_b[:, :], yc_b[:, :], AF.Square, [yc_b])
            pv_ = pbank()
            kb.mm(pv_, pv_[:, 0:TT], bones64, ysq_b[:, :], [cstR, ysq_b])
            kb.act(t1c_b, t1c_b[:, :], pv_[:, 0:TT], AF.Ln, [pv_, cstF], bias=gneps_c)
            yield
            kb.act(t1c_b, t1c_b[:, :], t1c_b[:, :], AF.Exp, [t1c_b], scale=-0.5)
            yield
            kb.tt(yc_b, yc_b[:, :], yc_b[:, :], t1c_b[:, :], ALU.mult, [yc_b, t1c_b])
            kb.ts(yc_b, yc_b[:, :], yc_b[:, :], pc("gnw", m), ALU.mult, [yc_b, par], s2=pc("gnb", m), op1=ALU.add)
            yield
            kb.tt(yc_b, yc_b[:, :], yc_b[:, :], bon_b[:, :], ALU.add, [yc_b, bon_b], eng="pool")
            kb.tt(ygb[m], ygb[m][:, :], yc_b[:, :], g_b[:, :], ALU.mult, [yc_b, g_b])
            yield

        def drain(g):
            for _ in g:
                pass

        def l0_mixer(tile_idx):
            kb.barrier()
            load_lora(0)
            normmod(0, sc1p, 0, hb, 1)
            for c in range(KC):
                kb.tt(qkpre[c], qkpre[c][:, 3:3 + TT], hb[c][:, 0:TT].bitcast(F32), hb[c][:, 1:TT + 1].bitcast(F32),
                      ALU.subtract, [hb[c]], eng="pool")
            x1 = mkxs(1, ygb)
            x4 = mkxs(4, xsb[2])
            x5 = mkxs(5, xsb[1])
            pb = pbank()
            inproj_fm(rw_w_in, 1024, 128, x1, pb)
            kb.act(wl, wl[:, :], pb[0:64, 0:TT], AF.Tanh, [pb])
            pb = pbank()
            inproj_fm(rw_w_in, 3136, 128, x4, pb)
            kb.cp(al, al[:, :], pb[0:64, 0:TT], [pb], eng="act")
            pb = pbank()
            inproj_fm(rw_w_in, 3200, 128, x5, pb)
            kb.act(gl, gl[:, 0, :], pb[:, 0:TT], AF.Sigmoid, [pb])
            pb = pbank()
            inproj_fm(rw_w_in, 3232, 128, x5, pb)
            kb.act(gl, gl[:, 1, :], pb[:, 0:TT], AF.Sigmoid, [pb])
            x0 = mkxs(0, xsb[0])
            x2 = mkxs(2, xsb[1])
            x3 = mkxs(3, xsb[2])
            for c in range(KC):
                kb.cp(hb[c], hb[c][:, 0:1], hb[c][:, TT:TT + 1], [hb[c]], eng="pool")
            drain(stage_A(0, x0, x2, x3))
            for m in range(8):
                fills = []
                if m > 0:
                    fills.append(stage_C(m - 1))
                if m + 1 < 8:
                    fills.append(stage_A(m + 1, x0, x2, x3))
                for _ in stage_B(m):
                    for f_ in list(fills):
                        try:
                            next(f_)
                        except StopIteration:
                            fills.remove(f_)
                for f_ in fills:
                    drain(f_)
            drain(stage_C(7))
            outproj(0, rw_w_out, ygb)

        aoff[0] = 0
        qk_t = aalloc([P, KC, TT])
        qkb = [Buf("qk%d" % c, qk_t[:, c, :]) for c in range(KC)]
        vT_t = aalloc([P, NCH, D])
        vTb = [Buf("vT%d" % c, vT_t[:, c, :]) for c in range(NCH)]
        oT_t = aalloc([P, NCH, D])
        oTb = [Buf("oT%d" % c, oT_t[:, c, :]) for c in range(NCH)]
        hgT_t = aalloc([P, NCH, D])
        hgTb = [Buf("hgT%d" % c, hgT_t[:, c, :]) for c in range(NCH)]
        gts = [asb("gts%d" % c, [P, 64]) for c in range(NCH)]
        NHF = 3
        Vg = [asb("Vg%d" % i, [P, 130]) for i in range(NHF)]
        STm = sm("STm", NHF)
        kTm = sm("kTm", 4)
        osb = [asb("osb%d" % i, [P, 136]) for i in range(NHF)]
        hhb = [asb("hhb%d" % i, [P, 128]) for i in range(NHF)]
        ctmp = [asb("ctmp%d" % i, [P, 130]) for i in range(NHF)]
        acc_b = wk("acc")[0]

        def l1_mixer(tile_idx):
            kb.barrier()
            normmod(1, sc1p, 0, hb, 1)
            for m in range(8):
                pb = pbank()
                inproj_fm(ml_w_in, m * 128, 128, [(hb[c], hb[c][:, 1:TT + 1]) for c in range(KC)], pb)
                kb.cp(qkpre[m], qkpre[m][:, 3:3 + TT], pb[:, 0:TT], [pb], eng="act")
                cw = lambda j: pc("cw", j * 8 + m)
                kb.ts(acc_b, acc_b[:, :], qkpre[m][:, 0:TT], cw(0), ALU.mult, [qkpre[m], par])
                for j in range(1, 4):
                    kb.stt(acc_b, acc_b[:, :], qkpre[m][:, j:j + TT], cw(j), acc_b[:, :], ALU.mult, ALU.add,
                           [qkpre[m], par, acc_b])
                kb.act(qkb[m], qkb[m][:, :], acc_b[:, :], AF.Silu, [acc_b, par], bias=pc("cb", m))
                if m < 4:
                    kb.ts(qkb[m], qkb[m][:, :], qkb[m][:, :], 0.125, ALU.mult, [qkb[m]], eng="pool")
                kb.cp(qkpre[m], qkpre[m][:, 0:3], qkpre[m][:, TT:TT + 3], [qkpre[m]], eng="pool")
            for blk in range(5):
                ncol = 512 if blk < 4 else 16
                c0 = 1024 + blk * 512
                wbufs, wap = wslot2()
                wv = wap.rearrange("p (k n) -> p k n", k=KC)
                kb.dma(wv[:, :, 0:ncol],
                       ml_w_in.rearrange("(k p) n -> p k n", p=P)[:, :, c0:c0 + ncol].bitcast(F32R),
                       writes=wbufs)
                for tb in range(NCH):
                    lhs = [(hb[kc], hb[kc][:, 1 + tb * L:1 + (tb + 1) * L]) for kc in range(KC)]
                    pb = pbank()
                    for kc in range(KC):
                        kb.mm(pb, pb[:, 0:ncol], lhs[kc][1], wv[:, kc, 0:ncol], wbufs + [lhs[kc][0]],
                              start=(kc == 0), stop=(kc == KC - 1))
                    if blk < 2:
                        kb.cp(vTb[tb], vTb[tb][:, blk * 512:(blk + 1) * 512], pb[:, 0:512], [pb])
                    elif blk < 4:
                        kb.act(oTb[tb], oTb[tb][:, (blk - 2) * 512:(blk - 1) * 512], pb[:, 0:512], AF.Sigmoid, [pb])
                    else:
                        G = gts[tb]
                        kb.tt(G, G[:, 0:8], pb[:, 0:8], parT[:, PART_OFF["bi"]:PART_OFF["bi"] + 8], ALU.add, [pb, parT])
                        kb.act(G, G[:, 0:8], G[:, 0:8], AF.Tanh, [G], scale=1.0 / 15.0)
                        kb.ts(G, G[:, 0:8], G[:, 0:8], 15.0, ALU.mult, [G])
                        kb.tt(G, G[:, 8:16], pb[:, 8:16], parT[:, PART_OFF["bf"]:PART_OFF["bf"] + 8], ALU.add, [pb, parT])
                        kb.act(G, G[:, 8:16], G[:, 8:16], AF.Tanh, [G], scale=1.0 / 15.0)
                        kb.act(G, G[:, 8:16], G[:, 8:16], AF.Sigmoid, [G], scale=15.0)
                        kb.act(G, G[:, 8:16], G[:, 8:16], AF.Ln, [G])
                        pf = pbank()
                        kb.mm(pf, pf[:, 0:8], m_iu, G[:, 8:16], [cstF, G])
                        kb.mm(pf, pf[:, 8:16], ones, G[:, 8:16], [cstF, G])
                        kb.cp(G, G[:, 16:24], pf[:, 0:8], [pf], eng="dve")
                        kb.act(G, G[:, 24:32], pf[:, 0:8], AF.Exp, [pf])
                        kb.act(G, G[:, 40:48], pf[:, 8:16], AF.Exp, [pf])
                        kb.tt(G, G[:, 32:40], G[:, 0:8], G[:, 16:24], ALU.subtract, [G])
                        kb.act(G, G[:, 32:40], G[:, 32:40], AF.Exp, [G])
            def head_gen(tb, hd):
                cs = slice(tb * L, (tb + 1) * L)
                G = gts[tb]
                pr_, hh = hd // 2, hd % 2
                hs = slice(hh * 64, (hh + 1) * 64)
                ix = hd % NHF
                es = slice(hd * 128, (hd + 1) * 128)
                C = Cst[pr_]
                kT = kTm[pr_]
                vg = Vg[ix]
                kb.ts(vg, vg[:, 0:128], vTb[tb][:, es], G[:, 32 + hd:33 + hd], ALU.mult, [vTb[tb], G])
                kb.cp(vg, vg[:, 128:129], G[:, 32 + hd:33 + hd], [G], eng="pool")
                pst = banks[2 * ix]
                kb.mm(pst, pst[:, 0:128], qkb[4 + pr_][hs, cs], qkb[pr_][hs, cs], [qkb[4 + pr_], qkb[pr_]])
                yield
                ST = STm[ix]
                kb.tt(ST, ST[:, :], pst[:, 0:128], m_iu, ALU.mult, [pst, cstF])
                pa_ = banks[2 * ix + 1]
                kb.mm(pa_, pa_[:, 0:129], qkb[pr_][hs, cs], C[hs, :], [qkb[pr_], C], start=True, stop=False)
                kb.mm(pa_, pa_[:, 0:129], ST[:, :], vg[:, 0:129], [ST, vg], start=False, stop=True)
                pc_ = banks[2 * ix]
                kb.mm(pc_, pc_[hs, 256:385], kT[:, hh * 64:(hh + 1) * 64], vg[:, 0:129], [kT, vg])
                yield
                ob = osb[ix]
                kb.ts(ob, ob[:, 0:129], pa_[:, 0:129], G[:, 24 + hd:25 + hd], ALU.mult, [pa_, G])
                ct = ctmp[ix]
                kb.tt(ct, ct[hs, 0:129], pc_[hs, 256:385], C[hs, :], ALU.add, [pc_, C])
                kb.ts(C, C[hs, :], ct[hs, 0:129], G[hs, 40 + hd:41 + hd], ALU.mult, [ct, G])
                yield
                kb.act(ob, ob[:, 129:130], ob[:, 128:129], AF.Abs, [ob])
                yield
                kb.ts(ob, ob[:, 129:130], ob[:, 129:130], 1.0, ALU.max, [ob])
                kb.op("dve", lambda e: e.reciprocal(out=ob[:, 130:131], in_=ob[:, 129:130]), reads=[ob], writes=[ob])
                yield
                hb_ = hhb[ix]
                kb.act(hb_, hb_[:, :], ob[:, 0:128], AF.Square, [ob], scale=ob[:, 130:131], accum_out=ob[:, 131:132])
                yield
                kb.act(ob, ob[:, 132:133], ob[:, 131:132], AF.Ln, [ob, cstF], scale=1.0 / 128.0, bias=eps_c)
                yield
                kb.act(ob, ob[:, 133:134], ob[:, 132:133], AF.Exp, [ob], scale=-0.5)
                kb.tt(ob, ob[:, 134:135], ob[:, 133:134], ob[:, 130:131], ALU.mult, [ob])
                kb.stt(hb_, hb_[:, :], ob[:, 0:128], ob[:, 134:135],
                       parT[:, PART_OFF["hnw"] + hd * 128:PART_OFF["hnw"] + (hd + 1) * 128],
                       ALU.mult, ALU.mult, [hb_, ob, parT])
                yield
                kb.tt(hgTb[tb], hgTb[tb][:, es], hb_[:, :], oTb[tb][:, es], ALU.mult, [hb_, oTb[tb]], eng="pool")
                yield

            def fm_gen(tb):
                cs = slice(tb * L, (tb + 1) * L)
                for kc in range(KC):
                    pt = (banks[6], ypsum)[kc % 2]
                    kb.tr(pt, pt[:, 0:128], hgTb[tb][:, kc * 128:(kc + 1) * 128], ident, [hgTb[tb], cstF])
                    kb.cp(ygb[kc], ygb[kc][:, cs], pt[:, 0:128], [pt])
                    yield

            def rr(gens):
                gens = list(gens)
                while gens:
                    for g_ in list(gens):
                        try:
                            next(g_)
                        except StopIteration:
                            gens.remove(g_)

            pend = []
            for tb in range(NCH):
                cs = slice(tb * L, (tb + 1) * L)
                for pr_ in range(4):
                    ptk = (banks[6], ypsum)[pr_ % 2]
                    kb.tr(ptk, ptk[:, 0:128], qkb[4 + pr_][:, cs], ident, [qkb[4 + pr_], cstF])
                    kb.cp(kTm[pr_], kTm[pr_][:, :], ptk[:, 0:128], [ptk])
                for g0 in range(0, 8, NHF):
                    gens = [head_gen(tb, hd) for hd in range(g0, min(8, g0 + NHF))]
                    if g0 == 0 and pend:
                        gens += pend
                        pend = []
                    rr(gens)
                pend = [fm_gen(tb)]
            rr(pend)
            outproj(1, ml_w_out, ygb)

        aoff[0] = 0
        ob_t = aalloc([P, KC, TT])
        obuf = Buf("obuf", ob_t)
        xT_v = xT.rearrange("(k p) t -> p k t", p=P)
        oT_v = outT.rearrange("(k p) t -> p k t", p=P)
        assert n_tiles % 2 == 0
        for tp in range(n_tiles // 2):
            for s_i in range(2):
                XO[0] = s_i * TT
                t0 = (2 * tp + s_i) * TT
                kb.dma(x_t[:, :, XO[0]:XO[0] + TT], xT_v[:, :, t0:t0 + TT], writes=xb)
                if "l0m" in stages:
                    l0_mixer(2 * tp + s_i)
            if "l0f" in stages:
                ffn(0)
            if "l1m" in stages:
                for s_i in range(2):
                    XO[0] = s_i * TT
                    l1_mixer(2 * tp + s_i)
            if "l1f" in stages:
                ffn(1)
            for s_i in range(2):
                XO[0] = s_i * TT
                t0 = (2 * tp + s_i) * TT
                if "fin" in stages:
                    kb.barrier()
                    ssb = pbank()
                    for c in range(KC):
                        s_ = sq[c % 2]
                        kb.act(s_, s_[:, :], xb[c][:, XO[0]:XO[0] + TT], AF.Square, [xb[c]])
                        kb.mm(ssb, ssb[:, 0:TT], onesD, s_[:, :], [s_, cstR], start=(c == 0), stop=(c == KC - 1))
                    kb.act(rs, rs[:, :], ssb[:, 0:TT], AF.Ln, [ssb, cstF], bias=eps_c)
                    kb.act(rs, rs[:, :], rs[:, :], AF.Exp, [rs], scale=-0.5)
                    for c in range(KC):
                        kb.stt(obuf, ob_t[:, c, :], xb[c][:, XO[0]:XO[0] + TT], pc("fw", c), rs[:, :],
                               ALU.mult, ALU.mult, [xb[c], rs, par])
                    kb.dma(oT_v[:, :, t0:t0 + TT], ob_t, reads=[obuf])
                else:
                    kb.dma(oT_v[:, :, t0:t0 + TT], x_t[:, :, XO[0]:XO[0] + TT], reads=xb)
        kb.finish()
        print("instructions", kb.ninst, "waits", kb.nwait, "arena", aoff[1], aoffR[0])
    return nc


TT_DEFAULT = 256


def make_in_maps(inp, TT, batches):
    cstF, cstR = make_consts(TT)
    parT = pack_part(inp)
    f = lambda a: np.ascontiguousarray(np.asarray(a, np.float32))
    shared = {
        "parT": parT, "cstF": cstF, "cstR": cstR,
        "ada_w": f(inp["ada_w"]), "rw_w_in": f(inp["rw_w_in"][0]), "rw_w2": f(inp["rw_w2"][0]),
        "rw_a2": f(inp["rw_a2"][0]), "rw_g2": f(inp["rw_g2"][0]), "rw_w_out": f(inp["rw_w_out"][0]),
        "ml_w_in": f(inp["ml_w_in"][0]), "ml_w_out": f(inp["ml_w_out"][0]),
        "ffn_w_gu": f(inp["ffn_w_gu"]), "ffn_w_down": f(inp["ffn_w_down"]),
    }
    maps = []
    for b in batches:
        m = dict(shared)
        m["xT"] = np.ascontiguousarray(np.asarray(inp["x"][b], np.float32).T)
        m["par"] = pack_par(inp, b)
        maps.append(m)
    return maps


def kernel(**inputs):
    TT = TT_DEFAULT
    nc = build_nc(TT=TT)
    batches = [0, 1, 2, 3, 0, 1, 2, 3]
    in_maps = make_in_maps(inputs, TT, batches)
    res = run_bass_kernel_spmd(nc, in_maps, core_ids=list(range(8)))
    out = np.stack([np.ascontiguousarray(res.results[b]["outT"].T) for b in range(4)], axis=0)
    return out.astype(np.float32)
```

```python
import contextlib
import os
import numpy as np
import concourse.bass as bass
import concourse.mybir as mybir
from concourse.bass_utils import run_bass_kernel_spmd

F32 = mybir.dt.float32
F32R = mybir.dt.float32r
BF16 = mybir.dt.bfloat16
AF = mybir.ActivationFunctionType
ALU = mybir.AluOpType

P = 128
D = 1024
KC = 8
S = 4096
DFF = 2816
RW_IN = 3360
ML_IN = 3088
EPS = 1e-6
GN_EPS = 64e-5
L = 128


class Buf:
    __slots__ = ("name", "ap", "w", "r")

    def __init__(self, name, ap):
        self.name = name
        self.ap = ap
        self.w = None
        self.r = {}

    def __getitem__(self, idx):
        return self.ap[idx]


class KB:
    def __init__(self, nc, stack, ndma=24):
        self.nc = nc
        self.stack = stack
        self.eng = {"pe": nc.tensor, "act": nc.scalar, "dve": nc.vector,
                    "pool": nc.gpsimd, "sp": nc.sync}
        self.sem = {}
        self.cnt = {}
        self.known = {e: {} for e in self.eng}
        for e in ["pe", "act", "dve", "pool"]:
            self.sem[e] = stack.enter_context(nc.semaphore("s_" + e))
            self.cnt[e] = 0
        self.ndma = ndma
        for i in range(ndma):
            self.sem[("d", i)] = stack.enter_context(nc.semaphore("d%d" % i))
            self.cnt[("d", i)] = 0
        self.dnext = 0
        self.hist = {}
        self.ninst = 0
        self.nwait = 0
        self.rr = 0

    def tile(self, name, shape, dtype=F32):
        return self.stack.enter_context(self.nc.sbuf_tensor("sb_" + name, list(shape), dtype))

    def sb(self, name, shape, dtype=F32):
        t = self.tile(name, shape, dtype)
        return Buf(name, t[:])

    def ps(self, name, shape, dtype=F32):
        t = self.stack.enter_context(self.nc.psum_tensor("ps_" + name, list(shape), dtype))
        return Buf(name, t[:])

    def _wait(self, e, deps):
        need = {}
        for k, v in deps:
            if v > need.get(k, 0):
                need[k] = v
        kn = self.known[e]
        for k, v in need.items():
            if e == "pe" and k == "pe":
                continue
            if kn.get(k, 0) >= v:
                continue
            self.eng[e].wait_ge(self.sem[k], v)
            kn[k] = v
            self.nwait += 1
            snap = self.hist.get((k, v))
            if snap:
                for k2, v2 in snap.items():
                    if kn.get(k2, 0) < v2:
                        kn[k2] = v2

    def _deps(self, reads, writes):
        deps = []
        for b in reads:
            if b.w:
                deps.append(b.w)
        for b in writes:
            if b.w:
                deps.append(b.w)
            deps.extend(b.r.items())
        return deps

    def _mark(self, tok, reads, writes):
        k, v = tok
        for b in reads:
            if b.r.get(k, 0) < v:
                b.r[k] = v
        for b in writes:
            b.w = tok
            b.r = {}

    def op(self, e, fn, reads=(), writes=()):
        self._wait(e, self._deps(reads, writes))
        inst = fn(self.eng[e])
        self.cnt[e] += 1
        inst.then_inc(self.sem[e], 1)
        tok = (e, self.cnt[e])
        self.hist[tok] = dict(self.known[e])
        self._mark(tok, reads, writes)
        self.ninst += 1
        return tok

    def dma(self, out, in_, reads=(), writes=(), q="sp"):
        i = self.dnext
        self.dnext = (self.dnext + 1) % self.ndma
        key = ("d", i)
        deps = self._deps(reads, writes)
        if self.cnt[key] > 0:
            deps.append((key, self.cnt[key]))
        self._wait(q, deps)
        inst = self.eng[q].dma_start(out=out, in_=in_)
        self.cnt[key] += 16
        inst.then_inc(self.sem[key], 16)
        tok = (key, self.cnt[key])
        self.hist[tok] = dict(self.known[q])
        self._mark(tok, reads, writes)
        self.ninst += 1
        return tok

    def barrier(self):
        toks = [(("d", i), self.cnt[("d", i)]) for i in range(self.ndma) if self.cnt[("d", i)] > 0]
        for k in ["pe", "act", "dve", "pool"]:
            if self.cnt[k] > 0:
                toks.append((k, self.cnt[k]))
        for e in ["pe", "act", "dve", "pool", "sp"]:
            self._wait(e, [t for t in toks if t[0] != e])

    def finish(self, e="sp"):
        toks = [(("d", i), self.cnt[("d", i)]) for i in range(self.ndma) if self.cnt[("d", i)] > 0]
        for k in ["pe", "act", "dve", "pool"]:
            if self.cnt[k] > 0:
                toks.append((k, self.cnt[k]))
        self._wait(e, toks)

    def mm(self, ob, out, lhsT, rhs, reads, start=True, stop=True):
        return self.op("pe", lambda e: e.matmul(out, lhsT, rhs, start=start, stop=stop),
                       reads=reads, writes=[ob])

    def tr(self, ob, out, in_, ident, reads):
        return self.op("pe", lambda e: e.transpose(out, in_, ident), reads=reads, writes=[ob])

    def tt(self, ob, out, in0, in1, op, reads, eng="dve"):
        return self.op(eng, lambda e: e.tensor_tensor(out=out, in0=in0, in1=in1, op=op),
                       reads=reads, writes=[ob])

    def ts(self, ob, out, in0, s1, op0, reads, s2=None, op1=None, eng="dve"):
        if op1 is None:
            return self.op(eng, lambda e: e.tensor_scalar(out=out, in0=in0, scalar1=s1, scalar2=None, op0=op0),
                           reads=reads, writes=[ob])
        return self.op(eng, lambda e: e.tensor_scalar(out=out, in0=in0, scalar1=s1, scalar2=s2, op0=op0, op1=op1),
                       reads=reads, writes=[ob])

    def stt(self, ob, out, in0, scalar, in1, op0, op1, reads):
        return self.op("dve", lambda e: e.scalar_tensor_tensor(out=out, in0=in0, scalar=scalar, in1=in1,
                                                                 op0=op0, op1=op1),
                       reads=reads, writes=[ob])

    def act(self, ob, out, in_, func, reads, bias=None, scale=1.0, accum_out=None):
        kw = {}
        if bias is not None:
            kw["bias"] = bias
        if accum_out is not None:
            kw["accum_out"] = accum_out
        return self.op("act", lambda e: e.activation(out=out, in_=in_, func=func, scale=scale, **kw),
                       reads=reads, writes=[ob])

    def cp(self, ob, out, in_, reads, eng=None):
        if eng is None:
            eng = ("act", "dve")[self.rr % 2]
            self.rr += 1
        if eng == "act":
            return self.op("act", lambda e: e.copy(out, in_), reads=reads, writes=[ob])
        return self.op(eng, lambda e: e.tensor_copy(out=out, in_=in_), reads=reads, writes=[ob])


def _fm(v):
    v = np.asarray(v, np.float32).reshape(-1)
    return np.ascontiguousarray(v.reshape(-1, 128).T)


PAR_LAYOUT = [("c", 8), ("adab0", 48), ("adab1", 48), ("mix", 48), ("w0", 8), ("a0", 8), ("kk", 8),
              ("ka", 8), ("rk", 8), ("gnw", 8), ("gnb", 8), ("cw", 32), ("cb", 8), ("fw", 8)]
PAR_OFF = {}
_o = 0
for _n, _w in PAR_LAYOUT:
    PAR_OFF[_n] = _o
    _o += _w
NPAR = _o

PART_LAYOUT = [("hnw", 1024), ("bi", 8), ("bf", 8)]
PART_OFF = {}
_o = 0
for _n, _w in PART_LAYOUT:
    PART_OFF[_n] = _o
    _o += _w
NPART = _o


def pack_par(inp, b):
    cols = [
        _fm(inp["c"][b]), _fm(inp["ada_b"][0]), _fm(inp["ada_b"][1]),
        np.concatenate([_fm(inp["rw_mix"][0, i]) for i in range(6)], axis=1),
        _fm(inp["rw_w0"][0]), _fm(inp["rw_a0"][0]), _fm(inp["rw_k_k"][0]), _fm(inp["rw_k_a"][0]),
        _fm(inp["rw_r_k"][0]), _fm(inp["rw_gn_w"][0]), _fm(inp["rw_gn_b"][0]),
        np.concatenate([_fm(inp["ml_conv_w"][0, j]) for j in range(4)], axis=1),
        _fm(inp["ml_conv_b"][0]), _fm(inp["final_w"]),
    ]
    out = np.concatenate(cols, axis=1).astype(np.float32)
    assert out.shape == (128, NPAR)
    return np.ascontiguousarray(out)


def pack_part(inp):
    row = np.concatenate([np.asarray(inp["ml_hn_w"][0], np.float32).reshape(-1),
                          np.asarray(inp["ml_b_i"][0], np.float32).reshape(-1),
                          np.asarray(inp["ml_b_f"][0], np.float32).reshape(-1)])
    return np.ascontiguousarray(np.broadcast_to(row[None, :], (128, NPART))).astype(np.float32)


def make_consts(TT):
    i = np.arange(128)
    ident = np.eye(128, dtype=np.float32)
    m_su = (i[:, None] < i[None, :]).astype(np.float32)
    m_sl = (i[:, None] > i[None, :]).astype(np.float32)
    m_iu = (i[:, None] <= i[None, :]).astype(np.float32)
    ones = np.ones((128, 128), np.float32)
    seg = np.ones((128, TT), np.float32)
    seg[:, ::L] = 0.0
    eps = np.zeros((128, 8), np.float32)
    eps[:, 0] = EPS
    eps[:, 1] = GN_EPS
    eps[:, 2] = np.log(0.125)
    rep4 = lambda a: np.concatenate([a] * 4, axis=1)
    cstF = np.concatenate([ident, m_su, m_sl, m_iu, ones, seg, eps, rep4(ident), rep4(m_su), rep4(m_sl), rep4(m_iu)],
                          axis=1)
    blk = (i[:, None] // 64 == i[None, :] // 64).astype(np.float32)
    cstR = np.concatenate([blk / 64.0, blk, ones / 1024.0], axis=1)
    return np.ascontiguousarray(cstF), np.ascontiguousarray(cstR)


def build_nc(TT=256, n_tiles=None, stages=("l0m", "l0f", "l1m", "l1f", "fin"), NSLOT=4):
    ARENA_WORDS = 12288 * TT // 256
    NCH = TT // L
    DBG = os.environ.get('KDBG', '').split(',')
    if n_tiles is None:
        n_tiles = S // TT
    nc = bass.Bass("TRN2", target_bir_lowering=False)
    nc.dge_precook = False
    dram = lambda n, sh, k="ExternalInput": nc.dram_tensor(n, list(sh), F32, kind=k).ap()
    xT = dram("xT", [D, S])
    par_d = dram("par", [P, NPAR])
    part_d = dram("parT", [P, NPART])
    NCF = 5 * 128 + TT + 8 + 4 * 512
    cstF_d = dram("cstF", [P, NCF])
    cstR_d = dram("cstR", [P, 3 * 128])
    ada_w = dram("ada_w", [2, D, 6 * D])
    rw_w_in = dram("rw_w_in", [D, RW_IN])
    rw_w2 = dram("rw_w2", [64, D])
    rw_a2 = dram("rw_a2", [64, D])
    rw_g2 = dram("rw_g2", [160, D])
    rw_w_out = dram("rw_w_out", [D, D])
    ml_w_in = dram("ml_w_in", [D, ML_IN])
    ml_w_out = dram("ml_w_out", [D, D])
    w_gu = dram("ffn_w_gu", [2, D, 2 * DFF])
    w_dn = dram("ffn_w_down", [2, DFF, D])
    outT = dram("outT", [D, S], "ExternalOutput")

    with contextlib.ExitStack() as st:
        kb = KB(nc, st)
        cstF = kb.sb("cstF", [P, NCF])
        cstR = kb.sb("cstR", [P, 3 * 128], F32R)
        par = kb.sb("par", [P, NPAR])
        parT = kb.sb("parT", [P, NPART])
        kb.dma(cstF[:], cstF_d, writes=[cstF])
        kb.dma(cstR[:], cstR_d.bitcast(F32R), writes=[cstR])
        kb.dma(par[:], par_d, writes=[par])
        kb.dma(parT[:], part_d, writes=[parT])
        ident = cstF[:, 0:128]
        m_su = cstF[:, 128:256]
        m_sl = cstF[:, 256:384]
        m_iu = cstF[:, 384:512]
        ones = cstF[:, 512:640]
        segm = cstF[:, 640:640 + TT]
        eps_c = cstF[:, 640 + TT:640 + TT + 1]
        gneps_c = cstF[:, 640 + TT + 1:640 + TT + 2]
        lnq_c = cstF[:, 640 + TT + 2:640 + TT + 3]
        c4 = 640 + TT + 8
        ident4 = cstF[:, c4:c4 + 512]
        m_su4 = cstF[:, c4 + 512:c4 + 1024]
        m_sl4 = cstF[:, c4 + 1024:c4 + 1536]
        m_iu4 = cstF[:, c4 + 1536:c4 + 2048]
        bones64 = cstR[:, 0:128]
        bones1 = cstR[:, 128:256]
        onesD = cstR[:, 256:384]
        pc = lambda n, j=0, w=1: par[:, PAR_OFF[n] + j:PAR_OFF[n] + j + w]

        der = kb.sb("der", [P, 2 * 48 + 2 * 16 + 8 + 8])
        modc = lambda i, j, w=1: der[:, i * 48 + j:i * 48 + j + w]
        sc1p = lambda i, c: der[:, 96 + i * 16 + c:96 + i * 16 + c + 1]
        sc2p = lambda i, c: der[:, 96 + i * 16 + 8 + c:96 + i * 16 + 8 + c + 1]
        oka = lambda m: der[:, 128 + m:128 + m + 1]
        cact = lambda kc: der[:, 136 + kc:136 + kc + 1]

        SLOTW = 2048
        slots_t = kb.tile("wslots", [P, NSLOT * SLOTW], F32R)
        slots = [Buf("ws%d" % i, slots_t[:, i * SLOTW:(i + 1) * SLOTW]) for i in range(NSLOT)]
        slot_i = [0]

        def wslot():
            s_ = slots[slot_i[0] % NSLOT]
            slot_i[0] += 1
            return s_

        def wslot2():
            if slot_i[0] % 2:
                slot_i[0] += 1
            i0 = slot_i[0] % NSLOT
            slot_i[0] += 2
            return [slots[i0], slots[i0 + 1]], slots_t[:, i0 * SLOTW:(i0 + 2) * SLOTW]

        NROT = 7
        banks = [kb.ps("pb%d" % i, [P, 512]) for i in range(NROT)]
        ypsum = kb.ps("ypsum", [P, 512])
        bank_i = [0]

        def pbank():
            b_ = banks[bank_i[0] % NROT]
            bank_i[0] += 1
            return b_

        ARW = ARENA_WORDS
        arena_t = kb.tile("arena", [P, ARW])
        ARWR = 6144 * TT // 256
        arenaR_t = kb.tile("arenaR", [P, ARWR], F32R)
        aoff = [0, 0]
        aoffR = [0]

        def aalloc(shape, dt=F32):
            words = 1
            for d_ in shape[1:]:
                words *= d_
            words += words % 2
            if dt == F32R:
                ap = arenaR_t[0:shape[0], aoffR[0]:aoffR[0] + words]
                aoffR[0] += words
                assert aoffR[0] <= ARWR, (aoffR[0], ARWR)
            else:
                ap = arena_t[0:shape[0], aoff[0]:aoff[0] + words]
                aoff[0] += words
                aoff[1] = max(aoff[1], aoff[0])
                assert aoff[0] <= ARW, (aoff[0], ARW)
            if len(shape) == 3:
                ap = ap.rearrange("p (a b) -> p a b", a=shape[1])
            return ap

        def asb(name, shape, dt=F32):
            return Buf(name, aalloc(shape, dt))

        x_t = kb.tile("x", [P, KC, 2 * TT])
        xb = [Buf("x%d" % c, x_t[:, c, :]) for c in range(KC)]
        XO = [0]
        h_t = kb.tile("h", [P, KC, TT + 1], F32R)
        hb = [Buf("h%d" % c, h_t[:, c, :]) for c in range(KC)]
        yg_t = kb.tile("yg", [P, KC, TT], F32R)
        ygb = [Buf("yg%d" % c, yg_t[:, c, :]) for c in range(KC)]
        xs_all = kb.tile("xs", [P, KC, 3 * TT], F32R)
        xsb = [[Buf("xs%d_%d" % (i, c), xs_all[:, c, i * TT:(i + 1) * TT]) for c in range(KC)] for i in range(3)]
        h2b = [Buf("h2_%d" % c, xs_all[:, c, 0:2 * TT]) for c in range(KC)]
        sq = [Buf("sq%d" % i, None) for i in range(2)]
        rs = kb.sb("rs", [P, TT])
        tmpn = [kb.sb("tmpn%d" % i, [P, TT]) for i in range(2)]

        for c in range(KC):
            kb.ts(hb[c], hb[c][:, 0:1], cstF[:, 0:1], 0.0, ALU.mult, [cstF])

        kb.act(der, der[:, 136:144], pc("c", 0, 8), AF.Silu, [par])
        for kc in range(KC):
            kb.ts(h2b[kc], h2b[kc][:, 0:128], cstF[:, 0:128], 0.0, ALU.mult, [cstF])
            kb.cp(h2b[kc], h2b[kc][:, 0:1], cact(kc), [der], eng="act")
        rowb = asb("rowb", [1, 512])
        for i in range(2):
            mp = ypsum
            for blk in range(12):
                wbufs, wap = wslot2()
                wvr = wap.rearrange("p (k n) -> p k n", k=KC)
                kb.dma(wvr, ada_w[i].rearrange("(k p) n -> p k n", p=P)[:, :, blk * 512:(blk + 1) * 512].bitcast(F32R),
                       writes=wbufs)
                pr_ = pbank()
                for kc in range(KC):
                    kb.mm(pr_, pr_[:, 0:512], h2b[kc][:, 0:128], wvr[:, kc, :], wbufs + [h2b[kc]],
                          start=(kc == 0), stop=(kc == KC - 1))
                kb.cp(rowb, rowb[0:1, :], pr_[0:1, 0:512], [pr_], eng="act")
                for jj in range(4):
                    j = blk * 4 + jj
                    kb.mm(mp, mp[:, j:j + 1], rowb[0:1, jj * 128:(jj + 1) * 128], ones[0:1, 0:1], [rowb, cstF])
            kb.tt(der, der[:, i * 48:(i + 1) * 48], mp[:, 0:48], pc("adab%d" % i, 0, 48), ALU.add, [mp, par])
            kb.ts(der, der[:, 96 + i * 16:96 + i * 16 + 8], modc(i, 8, 8), 1.0, ALU.add, [der])
            kb.ts(der, der[:, 96 + i * 16 + 8:96 + i * 16 + 16], modc(i, 32, 8), 1.0, ALU.add, [der])
        kb.ts(der, der[:, 128:136], pc("ka", 0, 8), -1.0, ALU.mult, [par], s2=1.0, op1=ALU.add)

        Hst = [kb.sb("H%d" % m, [P, 128]) for m in range(8)]
        Cst = [kb.sb("C%d" % m, [P, 129]) for m in range(4)]
        for b_ in Hst + Cst:
            kb.op("pool", lambda e: e.memset(b_[:, :], 0.0), writes=[b_])
        qkpre_t = kb.tile("qkpre", [P, KC, TT + 3])
        qkpre = [Buf("qkpre%d" % c, qkpre_t[:, c, :]) for c in range(KC)]
        kb.op("pool", lambda e: e.memset(qkpre_t[:, :, 0:3], 0.0), writes=qkpre)

        lw2 = [asb("lw2_%d" % i, [64, 128], F32R) for i in range(2)]
        la2 = [asb("la2_%d" % i, [64, 128], F32R) for i in range(2)]
        lg2 = [asb("lg2_%d" % i, [P, 2, 128], F32R) for i in range(2)]

        def load_lora(m):
            msl_ = slice(m * 128, (m + 1) * 128)
            kb.dma(lw2[m % 2][:, :], rw_w2[:, msl_].bitcast(F32R), writes=[lw2[m % 2]])
            kb.dma(la2[m % 2][:, :], rw_a2[:, msl_].bitcast(F32R), writes=[la2[m % 2]])
            kb.dma(lg2[m % 2][0:96, 0, :], rw_g2[0:96, msl_].bitcast(F32R), writes=[lg2[m % 2]])
            kb.dma(lg2[m % 2][64:128, 1, :], rw_g2[96:160, msl_].bitcast(F32R), writes=[lg2[m % 2]])

        def wk(name, n=1, dt=F32, shape=None):
            return [asb("%s%d" % (name, i), shape or [P, TT], dt) for i in range(n)]

        def normmod(i, scp, shoff, outb, outoff):
            ssb = pbank()
            for c in range(KC):
                s_ = sq[c % 2]
                kb.act(s_, s_[:, :], xb[c][:, XO[0]:XO[0] + TT], AF.Square, [xb[c]])
                kb.mm(ssb, ssb[:, 0:TT], onesD, s_[:, :], [s_, cstR], start=(c == 0), stop=(c == KC - 1))
            kb.act(rs, rs[:, :], ssb[:, 0:TT], AF.Ln, [ssb, cstF], bias=eps_c)
            kb.act(rs, rs[:, :], rs[:, :], AF.Exp, [rs], scale=-0.5)
            for c in range(KC):
                t_ = tmpn[c % 2]
                kb.stt(t_, t_[:, :], xb[c][:, XO[0]:XO[0] + TT], scp(i, c), rs[:, :], ALU.mult, ALU.mult, [xb[c], rs, der])
                kb.act(outb[c], outb[c][:, outoff:outoff + TT], t_[:, :], AF.Identity, [t_, der],
                       bias=modc(i, shoff + c))

        def outproj(i, w_d, srcb):
            for op_ in range(KC // 2):
                ws = wslot()
                wv = ws[:, 0:KC * 256].rearrange("p (k n) -> p k n", k=KC)
                kb.dma(wv, w_d.rearrange("(k p) n -> p k n", p=P)[:, :, op_ * 256:(op_ + 1) * 256].bitcast(F32R),
                       writes=[ws])
                for oo in range(2):
                    o = op_ * 2 + oo
                    pb = pbank()
                    for kc in range(KC):
                        kb.mm(pb, pb[:, 0:TT], wv[:, kc, oo * 128:(oo + 1) * 128], srcb[kc][:, 0:TT], [ws, srcb[kc]],
                              start=(kc == 0), stop=(kc == KC - 1))
                    kb.stt(xb[o], xb[o][:, XO[0]:XO[0] + TT], pb[:, 0:TT], modc(i, 16 + o),
                           xb[o][:, XO[0]:XO[0] + TT], ALU.mult, ALU.add, [pb, xb[o], der])

        TP = 2 * TT
        for i_ in range(2):
            sq[i_].ap = aalloc([P, TT], F32R)
        hff_t = aalloc([P, 4, TP], F32R)
        hff = [Buf("hff%d" % j, hff_t[:, j, :]) for j in range(4)]
        sgb = [asb("sgb%d" % i, [P, TP]) for i in range(2)]
        aoff[0] = 0

        def ffn(i):
            kb.barrier()
            for s_i in range(2):
                XO[0] = s_i * TT
                normmod(i, sc2p, 24, h2b, s_i * TT)
            ngroups = (DFF // 128 + 3) // 4
            for g in range(ngroups):
                nj = min(4, DFF // 128 - g * 4)
                src = w_gu[i].rearrange("(k p) n -> p k n", p=P)
                for jp in range(nj // 2):
                    f = g * 4 + jp * 2
                    wsA = wslot()
                    wvA = wsA[:, 0:KC * 256].rearrange("p (k n) -> p k n", k=KC)
                    kb.dma(wvA, src[:, :, f * 128:(f + 2) * 128].bitcast(F32R), writes=[wsA])
                    wsB = wslot()
                    wvB = wsB[:, 0:KC * 256].rearrange("p (k n) -> p k n", k=KC)
                    kb.dma(wvB, src[:, :, DFF + f * 128:DFF + (f + 2) * 128].bitcast(F32R), writes=[wsB])
                    for jj in range(2):
                        j = jp * 2 + jj
                        csl = slice(jj * 128, (jj + 1) * 128)
                        pg = pbank()
                        for kc in range(KC):
                            kb.mm(pg, pg[:, 0:TP], wvA[:, kc, csl], h2b[kc][:, :], [wsA, h2b[kc]],
                                  start=(kc == 0), stop=(kc == KC - 1))
                        pu = pbank()
                        for kc in range(KC):
                            kb.mm(pu, pu[:, 0:TP], wvB[:, kc, csl], h2b[kc][:, :], [wsB, h2b[kc]],
                                  start=(kc == 0), stop=(kc == KC - 1))
                        s_ = sgb[j % 2]
                        kb.act(s_, s_[:, :], pg[:, 0:TP], AF.Silu, [pg])
                        kb.tt(hff[j], hff[j][:, :], s_[:, :], pu[:, 0:TP], ALU.mult, [s_, pu])
                for half in range(2):
                    ws = wslot()
                    wv = ws[:, 0:nj * 512].rearrange("p (k n) -> p k n", k=nj)
                    kb.dma(wv, w_dn[i][g * 512:g * 512 + nj * 128, :].rearrange("(k p) n -> p k n", p=P)[
                        :, :, half * 512:(half + 1) * 512].bitcast(F32R), writes=[ws])
                    for oo in range(4):
                        o = half * 4 + oo
                        pb = pbank()
                        for j in range(nj):
                            kb.mm(pb, pb[:, 0:TP], wv[:, j, oo * 128:(oo + 1) * 128], hff[j][:, :],
                                  [ws, hff[j]], start=(j == 0), stop=(j == nj - 1))
                        kb.stt(xb[o], xb[o][:, :], pb[:, 0:TP], modc(i, 40 + o), xb[o][:, :], ALU.mult, ALU.add,
                               [pb, xb[o], der])

        wl = asb("wl", [64, TT], F32R)
        al = asb("al", [64, TT], F32R)
        gl = asb("gl", [P, 2, TT], F32R)

        def wk(name, n=1, dt=F32, shape=None):
            return [asb("%s%d" % (name, i), shape or [P, TT], dt) for i in range(n)]

        r_b, k0_b, k_b = wk("r")[0], wk("k0")[0], wk("k")[0]
        lw_b, cl_b, asg_b = wk("lw")[0], wk("cl")[0], wk("asg")[0]
        kkn_b, b_b = wk("kkn")[0], wk("bb")[0]
        kksq_b = wk("kksq", 1, F32R)[0]
        e2_b, e3_b = wk("e2")[0], wk("e3")[0]
        t1_b, t2_b = wk("t1")[0], wk("t2")[0]
        rkp_b = wk("rkp", 1, F32R)[0]
        DB = []
        for i_ in range(2):
            d_ = {}
            for nm in ("At", "Rt", "v", "Bp", "Kp", "e1", "bon", "g"):
                d_[nm] = asb("%s_%d" % (nm, i_), [P, TT])
            for nm in ("Atb", "Btb", "Ktb", "Rtb"):
                d_[nm] = kb.sb("%s_%d" % (nm, i_), [P, TT], BF16)
            d_["ys"] = asb("ys_%d" % i_, [P, TT], F32R)
            DB.append(d_)
        yc_b = wk("yc")[0]
        ysq_b = wk("ysq", 1, F32R)[0]
        t1c_b = wk("t1c")[0]
        sm = lambda name, n, w=128: [asb("%s%d" % (name, i), [P, w]) for i in range(n)]
        NCHAIN = 4
        big = lambda name, w=NCHAIN * 128: asb(name, [P, w])
        S1 = [big("S1_%d" % i) for i in range(2)]
        S2 = [big("S2_%d" % i) for i in range(2)]
        bfb = lambda name, w=NCHAIN * 128: kb.sb(name, [P, w], BF16)
        S3 = [bfb("S3_%d" % i, 256) for i in range(2)]
        P0b = [bfb("P0b_%d" % i, 256) for i in range(2)]
        PmA, PTmA, PmB, PTmB = bfb("PmA"), bfb("PTmA"), bfb("PmB"), bfb("PTmB")
        QmA = big("QmA")
        Qbf = bfb("Qbf")
        TM3 = [asb("TM3_%d" % i, [P, 3 * 128]) for i in range(2)]
        Zs, Us = sm("Zs", 2, 128), sm("Us", 2, 128)

        def inproj_fm(w_d, col0, n, rhs_list, pb, mix=None):
            ws = wslot()
            nk = len(rhs_list)
            wv = ws[:, 0:KC * 128].rearrange("p (k n) -> p k n", k=KC)
            kb.dma(wv[:, 0:KC, 0:n], w_d.rearrange("(k p) n -> p k n", p=P)[:, :, col0:col0 + n].bitcast(F32R),
                   writes=[ws])
            for kk_ in range(nk):
                rb, rap = rhs_list[kk_]
                kb.mm(pb, pb[0:n, 0:TT], wv[:, kk_, 0:n], rap, [ws, rb], start=(kk_ == 0), stop=(kk_ == nk - 1))

        def mkxs(mix, dst):
            for c in range(KC):
                kb.stt(dst[c], dst[c][:, 0:TT], qkpre[c][:, 3:3 + TT], pc("mix", mix * 8 + c),
                       hb[c][:, 1:TT + 1].bitcast(F32), ALU.mult, ALU.add, [qkpre[c], hb[c], par])
            return [(dst[c], dst[c][:, 0:TT]) for c in range(KC)]

        def stage_A(m, x0, x2, x3):
            D_ = DB[m % 2]
            At_b, Rt_b, v_b, Bp_b, Kp_b, e1_b, bon_b, g_b = (D_["At"], D_["Rt"], D_["v"], D_["Bp"], D_["Kp"],
                                                                   D_["e1"], D_["bon"], D_["g"])
            Atb, Btb, Ktb, Rtb = D_["Atb"], D_["Btb"], D_["Ktb"], D_["Rtb"]
            if m + 1 < 8:
                load_lora(m + 1)
            pr = pbank()
            inproj_fm(rw_w_in, m * 128, 128, x0, pr)
            kb.cp(r_b, r_b[:, :], pr[:, 0:TT], [pr], eng="act")
            yield
            pk = pbank()
            inproj_fm(rw_w_in, 1088 + m * 128, 128, x2, pk)
            kb.cp(k0_b, k0_b[:, :], pk[:, 0:TT], [pk], eng="dve")
            yield
            pv = pbank()
            inproj_fm(rw_w_in, 2112 + m * 128, 128, x3, pv)
            kb.cp(v_b, v_b[:, :], pv[:, 0:TT], [pv], eng="act")
            yield
            pw = pbank()
            kb.mm(pw, pw[:, 0:TT], lw2[m % 2][:, :], wl[:, :], [lw2[m % 2], wl])
            kb.act(lw_b, lw_b[:, :], pw[:, 0:TT], AF.Sigmoid, [pw, par], bias=pc("w0", m))
            pa = pbank()
            kb.mm(pa, pa[:, 0:TT], la2[m % 2][:, :], al[:, :], [la2[m % 2], al])
            kb.act(asg_b, asg_b[:, :], pa[:, 0:TT], AF.Sigmoid, [pa, par], bias=pc("a0", m))
            yield
            kb.ts(lw_b, lw_b[:, :], lw_b[:, :], -0.6065306597126334, ALU.mult, [lw_b], eng="dve")
            pg_ = pbank()
            kb.mm(pg_, pg_[:, 0:TT], lg2[m % 2][0:96, 0, :], gl[0:96, 0, :], [lg2[m % 2], gl], start=True, stop=False)
            kb.mm(pg_, pg_[:, 0:TT], lg2[m % 2][64:128, 1, :], gl[64:128, 1, :], [lg2[m % 2], gl], start=False, stop=True)
            kb.cp(g_b, g_b[:, :], pg_[:, 0:TT], [pg_], eng="act")
            yield
            kb.ts(kkn_b, kkn_b[:, :], k0_b[:, :], pc("kk", m), ALU.mult, [k0_b, par])
            kb.act(kksq_b, kksq_b[:, :], kkn_b[:, :], AF.Square, [kkn_b])
            pn = pbank()
            kb.mm(pn, pn[:, 0:TT], bones1, kksq_b[:, :], [cstR, kksq_b])
            kb.ts(t1_b, t1_b[:, :], pn[:, 0:TT], 1e-18, ALU.max, [pn])
            yield
            kb.op("dve", lambda e: e.tensor_tensor_scan(out=cl_b[:, :], data0=segm, data1=lw_b[:, :],
                                                          initial=0.0, op0=ALU.mult, op1=ALU.add),
                  reads=[cstF, lw_b], writes=[cl_b])
            yield
            kb.act(t1_b, t1_b[:, :], t1_b[:, :], AF.Ln, [t1_b])
            kb.act(t1_b, t1_b[:, :], t1_b[:, :], AF.Exp, [t1_b], scale=-0.5)
            yield
            kb.tt(kkn_b, kkn_b[:, :], kkn_b[:, :], t1_b[:, :], ALU.mult, [kkn_b, t1_b])
            kb.ts(t2_b, t2_b[:, :], asg_b[:, :], pc("ka", m), ALU.mult, [asg_b, par, der], s2=oka(m), op1=ALU.add)
            yield
            kb.tt(k_b, k_b[:, :], k0_b[:, :], t2_b[:, :], ALU.mult, [k0_b, t2_b])
            kb.tt(b_b, b_b[:, :], kkn_b[:, :], asg_b[:, :], ALU.mult, [kkn_b, asg_b], eng="pool")
            yield
            kb.act(e1_b, e1_b[:, :], cl_b[:, :], AF.Exp, [cl_b])
            kb.act(e2_b, e2_b[:, :], cl_b[:, :], AF.Exp, [cl_b], scale=-1.0)
            kb.tt(t1_b, t1_b[:, :], cl_b[:, :], lw_b[:, :], ALU.subtract, [cl_b, lw_b], eng="pool")
            yield
            kb.act(e3_b, e3_b[:, :], t1_b[:, :], AF.Exp, [t1_b])
            kb.tt(Btb, Btb[:, :], b_b[:, :], e2_b[:, :], ALU.mult, [b_b, e2_b])
            kb.tt(Ktb, Ktb[:, :], k_b[:, :], e2_b[:, :], ALU.mult, [k_b, e2_b], eng="pool")
            yield
            kb.tt(Rt_b, Rt_b[:, :], r_b[:, :], e1_b[:, :], ALU.mult, [r_b, e1_b])
            kb.cp(Rtb, Rtb[:, :], Rt_b[:, :], [Rt_b], eng="pool")
            kb.stt(At_b, At_b[:, :], kkn_b[:, :], -1.0, e3_b[:, :], ALU.mult, ALU.mult, [kkn_b, e3_b])
            kb.cp(Atb, Atb[:, :], At_b[:, :], [At_b], eng="pool")
            yield
            for c in range(NCH):
                cs = slice(c * L, (c + 1) * L)
                kb.act(e3_b, e3_b[:, cs], cl_b[:, cs], AF.Exp, [cl_b], scale=-1.0,
                       bias=cl_b[:, (c + 1) * L - 1:(c + 1) * L])
            yield
            kb.tt(Bp_b, Bp_b[:, :], b_b[:, :], e3_b[:, :], ALU.mult, [b_b, e3_b])
            kb.tt(Kp_b, Kp_b[:, :], k_b[:, :], e3_b[:, :], ALU.mult, [k_b, e3_b], eng="pool")
            yield
            kb.stt(rkp_b, rkp_b[:, :], r_b[:, :], pc("rk", m), k_b[:, :], ALU.mult, ALU.mult, [r_b, k_b, par])
            pbn = pbank()
            kb.mm(pbn, pbn[:, 0:TT], bones1, rkp_b[:, :], [cstR, rkp_b])
            kb.tt(bon_b, bon_b[:, :], pbn[:, 0:TT], v_b[:, :], ALU.mult, [pbn, v_b])
            yield

        def stage_B(m):
            D_ = DB[m % 2]
            At_b, Rt_b, v_b, Bp_b, Kp_b, e1_b = D_["At"], D_["Rt"], D_["v"], D_["Bp"], D_["Kp"], D_["e1"]
            Atb, Btb, Ktb, Rtb, ysb = D_["Atb"], D_["Btb"], D_["Ktb"], D_["Rtb"], D_["ys"]
            H = Hst[m]
            for g0 in range(0, NCH, 2):
                gchunks = list(range(g0, min(g0 + 2, NCH)))
                ncg = len(gchunks)
                for c in gchunks:
                    cs = slice(c * L, (c + 1) * L)
                    pt = pbank()
                    for j_, src in enumerate((v_b, Bp_b, Kp_b)):
                        kb.tr(pt, pt[:, j_ * 128:(j_ + 1) * 128], src[:, cs], ident, [src, cstF])
                    kb.cp(TM3[c % 2], TM3[c % 2][:, :], pt[:, 0:384], [pt])
                yield
                def score_bank(hh, pairs, dst, msk4):
                    hs = slice(hh * 64, (hh + 1) * 64)
                    pp = pbank()
                    W_ = len(pairs) * 256
                    for j_, (lb, rb_) in enumerate(pairs):
                        for ci, c in enumerate(gchunks):
                            cs = slice(c * L, (c + 1) * L)
                            o_ = j_ * 256 + ci * 128
                            kb.mm(pp, pp[:, o_:o_ + 128], lb[hs, cs], rb_[hs, cs], [lb, rb_])
                    if ncg == 2:
                        kb.tt(dst, dst[:, 0:W_], pp[:, 0:W_], msk4[:, 0:W_], ALU.mult, [pp, cstF])
                    else:
                        for j_ in range(len(pairs)):
                            kb.tt(dst, dst[:, j_ * 256:j_ * 256 + 128], pp[:, j_ * 256:j_ * 256 + 128],
                                  msk4[:, 0:128], ALU.mult, [pp, cstF])

                for hh in range(2):
                    score_bank(hh, ((Btb, Atb), (Ktb, Atb)), S1[hh], m_su4)
                    score_bank(hh, ((Atb, Btb),), S3[hh], m_sl4)
                    yield
                chains = [(hh, ci) for hh in range(2) for ci in range(ncg)]
                ks = lambda k_: slice(k_ * 128, (k_ + 1) * 128)
                kq = lambda hh, ci: hh * 2 + ci
                Pc = [(P0b[hh], slice(ci * 128, (ci + 1) * 128)) for hh, ci in chains]
                PTc = [(S3[hh], slice(ci * 128, (ci + 1) * 128)) for hh, ci in chains]
                for hh in range(2):
                    kb.tt(QmA, QmA[:, hh * 256:hh * 256 + ncg * 128], S1[hh][:, 0:ncg * 128],
                          ident4[:, 0:ncg * 128], ALU.add, [S1[hh], cstF], eng="pool")
                    kb.cp(P0b[hh], P0b[hh][:, 0:ncg * 128], S1[hh][:, 0:ncg * 128], [S1[hh]], eng="act")
                kb.cp(Qbf, Qbf[:, :], QmA[:, :], [QmA], eng="dve")
                yield
                for hh in range(2):
                    score_bank(hh, ((Btb, Rtb), (Ktb, Rtb)), S2[hh], m_iu4)
                yield
                for it in range(1, 7):
                    last = (it == 6)
                    b1 = pbank()
                    for (hh, ci), (pb_, psl), (ptb_, ptsl) in zip(chains, Pc, PTc):
                        kb.mm(b1, b1[:, ks(kq(hh, ci))], pb_[:, psl], ptb_[:, ptsl], [pb_, ptb_])
                    if not last:
                        b2 = pbank()
                        for (hh, ci), (pb_, psl), (ptb_, ptsl) in zip(chains, Pc, PTc):
                            kb.mm(b2, b2[:, ks(kq(hh, ci))], ptb_[:, ptsl], pb_[:, psl], [pb_, ptb_])
                    nPT = PTmA if (it % 2 == 1) else PTmB
                    kb.cp(nPT, nPT[:, :], b1[:, :], [b1], eng="act")
                    PTc = [(nPT, ks(kq(hh, ci))) for hh, ci in chains]
                    if not last:
                        nP = PmA if (it % 2 == 1) else PmB
                        kb.cp(nP, nP[:, :], b2[:, :], [b2], eng="act")
                        Pc = [(nP, ks(kq(hh, ci))) for hh, ci in chains]
                    yield
                    b3 = pbank()
                    for (hh, ci), (ptb_, ptsl) in zip(chains, PTc):
                        kb.mm(b3, b3[:, ks(kq(hh, ci))], ptb_[:, ptsl], Qbf[:, ks(kq(hh, ci))], [ptb_, Qbf])
                    if not last:
                        kb.tt(Qbf, Qbf[:, :], b3[:, :], QmA[:, :], ALU.add, [b3, QmA])
                    kb.tt(QmA, QmA[:, :], b3[:, :], QmA[:, :], ALU.add, [b3, QmA])
                    yield
                for ci, c in enumerate(gchunks):
                    cs = slice(c * L, (c + 1) * L)
                    tm = TM3[c % 2]
                    pz = pbank()
                    kb.mm(pz, pz[:, 0:128], At_b[:, cs], H[:, :], [At_b, H], start=True, stop=False)
                    for hh in range(2):
                        vs = slice(hh * 64, (hh + 1) * 64)
                        kb.mm(pz, pz[:, vs], S1[hh][:, 256 + ci * 128:256 + (ci + 1) * 128], tm[:, vs],
                              [S1[hh], tm], start=False, stop=(hh == 1))
                    zsb = Zs[c % 2]
                    kb.cp(zsb, zsb[:, :], pz[:, 0:128], [pz], eng="act")
                    yield
                    pu_ = pbank()
                    for hh in range(2):
                        vs = slice(hh * 64, (hh + 1) * 64)
                        kb.mm(pu_, pu_[:, vs], QmA[:, ks(kq(hh, ci))], zsb[:, vs], [QmA, zsb])
                    usb = Us[c % 2]
                    kb.cp(usb, usb[:, :], pu_[:, 0:128], [pu_], eng="dve")
                    yield
                    kb.mm(ypsum, ypsum[:, cs], H[:, :], Rt_b[:, cs], [H, Rt_b], start=True, stop=False)
                    ph = pbank()
                    for hh in range(2):
                        hs = slice(hh * 64, (hh + 1) * 64)
                        vs = slice(hh * 64, (hh + 1) * 64)
                        kb.mm(ph, ph[hs, 0:64], tm[:, 128 + hh * 64:128 + (hh + 1) * 64], usb[:, vs], [tm, usb],
                              start=True, stop=False)
                        kb.mm(ph, ph[hs, 0:64], tm[:, 256 + hh * 64:256 + (hh + 1) * 64], tm[:, vs], [tm],
                              start=False, stop=True)
                    for hh in range(2):
                        hs = slice(hh * 64, (hh + 1) * 64)
                        vs = slice(hh * 64, (hh + 1) * 64)
                        kb.stt(H, H[hs, vs], H[hs, vs], e1_b[hs, (c + 1) * L - 1:(c + 1) * L], ph[hs, 0:64],
                               ALU.mult, ALU.add, [H, e1_b, ph])
                    for hh in range(2):
                        hs = slice(hh * 64, (hh + 1) * 64)
                        vs = slice(hh * 64, (hh + 1) * 64)
                        yo = ypsum[hs, cs]
                        kb.mm(ypsum, yo, usb[:, vs], S2[hh][:, ci * 128:(ci + 1) * 128], [usb, S2[hh]],
                              start=False, stop=False)
                        kb.mm(ypsum, yo, tm[:, vs], S2[hh][:, 256 + ci * 128:256 + (ci + 1) * 128], [tm, S2[hh]],
                              start=False, stop=True)
                    yield
            kb.cp(ysb, ysb[:, :], ypsum[:, 0:TT], [ypsum], eng="act")
            yield

        def stage_C(m):
            D_ = DB[m % 2]
            ysb, bon_b, g_b = D_["ys"], D_["bon"], D_["g"]
            pm_ = pbank()
            kb.mm(pm_, pm_[:, 0:TT], bones64, ysb[:, :], [cstR, ysb])
            kb.tt(yc_b, yc_b[:, :], ysb[:, :].bitcast(F32), pm_[:, 0:TT], ALU.subtract, [ysb, pm_])
            yield
            kb.act(ysq_b, ysq_b[:, :], yc_b[:, :], AF.Square, [yc_b])
            pv_ = pbank()
            kb.mm(pv_, pv_[:, 0:TT], bones64, ysq_b[:, :], [cstR, ysq_b])
            kb.act(t1c_b, t1c_b[:, :], pv_[:, 0:TT], AF.Ln, [pv_, cstF], bias=gneps_c)
            yield
            kb.act(t1c_b, t1c_b[:, :], t1c_b[:, :], AF.Exp, [t1c_b], scale=-0.5)
            yield
            kb.tt(yc_b, yc_b[:, :], yc_b[:, :], t1c_b[:, :], ALU.mult, [yc_b, t1c_b])
            kb.ts(yc_b, yc_b[:, :], yc_b[:, :], pc("gnw", m), ALU.mult, [yc_b, par], s2=pc("gnb", m), op1=ALU.add)
            yield
            kb.tt(yc_b, yc_b[:, :], yc_b[:, :], bon_b[:, :], ALU.add, [yc_b, bon_b], eng="pool")
            kb.tt(ygb[m], ygb[m][:, :], yc_b[:, :], g_b[:, :], ALU.mult, [yc_b, g_b])
            yield

        def drain(g):
            for _ in g:
                pass

        def l0_mixer(tile_idx):
            kb.barrier()
            load_lora(0)
            normmod(0, sc1p, 0, hb, 1)
            for c in range(KC):
                kb.tt(qkpre[c], qkpre[c][:, 3:3 + TT], hb[c][:, 0:TT].bitcast(F32), hb[c][:, 1:TT + 1].bitcast(F32),
                      ALU.subtract, [hb[c]], eng="pool")
            x1 = mkxs(1, ygb)
            x4 = mkxs(4, xsb[2])
            x5 = mkxs(5, xsb[1])
            pb = pbank()
            inproj_fm(rw_w_in, 1024, 128, x1, pb)
            kb.act(wl, wl[:, :], pb[0:64, 0:TT], AF.Tanh, [pb])
            pb = pbank()
            inproj_fm(rw_w_in, 3136, 128, x4, pb)
            kb.cp(al, al[:, :], pb[0:64, 0:TT], [pb], eng="act")
            pb = pbank()
            inproj_fm(rw_w_in, 3200, 128, x5, pb)
            kb.act(gl, gl[:, 0, :], pb[:, 0:TT], AF.Sigmoid, [pb])
            pb = pbank()
            inproj_fm(rw_w_in, 3232, 128, x5, pb)
            kb.act(gl, gl[:, 1, :], pb[:, 0:TT], AF.Sigmoid, [pb])
            x0 = mkxs(0, xsb[0])
            x2 = mkxs(2, xsb[1])
            x3 = mkxs(3, xsb[2])
            for c in range(KC):
                kb.cp(hb[c], hb[c][:, 0:1], hb[c][:, TT:TT + 1], [hb[c]], eng="pool")
            drain(stage_A(0, x0, x2, x3))
            for m in range(8):
                fills = []
                if m > 0:
                    fills.append(stage_C(m - 1))
                if m + 1 < 8:
                    fills.append(stage_A(m + 1, x0, x2, x3))
                for _ in stage_B(m):
                    for f_ in list(fills):
                        try:
                            next(f_)
                        except StopIteration:
                            fills.remove(f_)
                for f_ in fills:
                    drain(f_)
            drain(stage_C(7))
            outproj(0, rw_w_out, ygb)

        aoff[0] = 0
        qk_t = aalloc([P, KC, TT])
        qkb = [Buf("qk%d" % c, qk_t[:, c, :]) for c in range(KC)]
        vT_t = aalloc([P, NCH, D])
        vTb = [Buf("vT%d" % c, vT_t[:, c, :]) for c in range(NCH)]
        oT_t = aalloc([P, NCH, D])
        oTb = [Buf("oT%d" % c, oT_t[:, c, :]) for c in range(NCH)]
        hgT_t = aalloc([P, NCH, D])
        hgTb = [Buf("hgT%d" % c, hgT_t[:, c, :]) for c in range(NCH)]
        gts = [asb("gts%d" % c, [P, 64]) for c in range(NCH)]
        NHF = 3
        Vg = [asb("Vg%d" % i, [P, 130]) for i in range(NHF)]
        STm = sm("STm", NHF)
        kTm = sm("kTm", 4)
        osb = [asb("osb%d" % i, [P, 136]) for i in range(NHF)]
        hhb = [asb("hhb%d" % i, [P, 128]) for i in range(NHF)]
        ctmp = [asb("ctmp%d" % i, [P, 130]) for i in range(NHF)]
        acc_b = wk("acc")[0]

        def l1_mixer(tile_idx):
            kb.barrier()
            normmod(1, sc1p, 0, hb, 1)
            for m in range(8):
                pb = pbank()
                inproj_fm(ml_w_in, m * 128, 128, [(hb[c], hb[c][:, 1:TT + 1]) for c in range(KC)], pb)
                kb.cp(qkpre[m], qkpre[m][:, 3:3 + TT], pb[:, 0:TT], [pb], eng="act")
                cw = lambda j: pc("cw", j * 8 + m)
                kb.ts(acc_b, acc_b[:, :], qkpre[m][:, 0:TT], cw(0), ALU.mult, [qkpre[m], par])
                for j in range(1, 4):
                    kb.stt(acc_b, acc_b[:, :], qkpre[m][:, j:j + TT], cw(j), acc_b[:, :], ALU.mult, ALU.add,
                           [qkpre[m], par, acc_b])
                kb.act(qkb[m], qkb[m][:, :], acc_b[:, :], AF.Silu, [acc_b, par], bias=pc("cb", m))
                kb.cp(qkpre[m], qkpre[m][:, 0:3], qkpre[m][:, TT:TT + 3], [qkpre[m]], eng="pool")
            for blk in range(5):
                ncol = 512 if blk < 4 else 16
                c0 = 1024 + blk * 512
                wbufs, wap = wslot2()
                wv = wap.rearrange("p (k n) -> p k n", k=KC)
                kb.dma(wv[:, :, 0:ncol],
                       ml_w_in.rearrange("(k p) n -> p k n", p=P)[:, :, c0:c0 + ncol].bitcast(F32R),
                       writes=wbufs)
                for tb in range(NCH):
                    lhs = [(hb[kc], hb[kc][:, 1 + tb * L:1 + (tb + 1) * L]) for kc in range(KC)]
                    pb = pbank()
                    for kc in range(KC):
                        kb.mm(pb, pb[:, 0:ncol], lhs[kc][1], wv[:, kc, 0:ncol], wbufs + [lhs[kc][0]],
                              start=(kc == 0), stop=(kc == KC - 1))
                    if blk < 2:
                        kb.cp(vTb[tb], vTb[tb][:, blk * 512:(blk + 1) * 512], pb[:, 0:512], [pb])
                    elif blk < 4:
                        kb.act(oTb[tb], oTb[tb][:, (blk - 2) * 512:(blk - 1) * 512], pb[:, 0:512], AF.Sigmoid, [pb])
                    else:
                        G = gts[tb]
                        kb.tt(G, G[:, 0:8], pb[:, 0:8], parT[:, PART_OFF["bi"]:PART_OFF["bi"] + 8], ALU.add, [pb, parT])
                        kb.act(G, G[:, 0:8], G[:, 0:8], AF.Tanh, [G], scale=1.0 / 15.0)
                        kb.ts(G, G[:, 0:8], G[:, 0:8], 15.0, ALU.mult, [G])
                        kb.tt(G, G[:, 8:16], pb[:, 8:16], parT[:, PART_OFF["bf"]:PART_OFF["bf"] + 8], ALU.add, [pb, parT])
                        kb.act(G, G[:, 8:16], G[:, 8:16], AF.Tanh, [G], scale=1.0 / 15.0)
                        kb.act(G, G[:, 8:16], G[:, 8:16], AF.Sigmoid, [G], scale=15.0)
                        kb.act(G, G[:, 8:16], G[:, 8:16], AF.Ln, [G])
                        pf = pbank()
                        kb.mm(pf, pf[:, 0:8], m_iu, G[:, 8:16], [cstF, G])
                        kb.mm(pf, pf[:, 8:16], ones, G[:, 8:16], [cstF, G])
                        kb.cp(G, G[:, 16:24], pf[:, 0:8], [pf], eng="dve")
                        kb.act(G, G[:, 24:32], pf[:, 0:8], AF.Exp, [pf, cstF], bias=lnq_c)
                        kb.act(G, G[:, 40:48], pf[:, 8:16], AF.Exp, [pf])
                        kb.tt(G, G[:, 32:40], G[:, 0:8], G[:, 16:24], ALU.subtract, [G])
                        kb.act(G, G[:, 32:40], G[:, 32:40], AF.Exp, [G])
            def head_gen(tb, hd):
                cs = slice(tb * L, (tb + 1) * L)
                G = gts[tb]
                pr_, hh = hd // 2, hd % 2
                hs = slice(hh * 64, (hh + 1) * 64)
                ix = hd % NHF
                es = slice(hd * 128, (hd + 1) * 128)
                C = Cst[pr_]
                kT = kTm[pr_]
                vg = Vg[ix]
                kb.ts(vg, vg[:, 0:128], vTb[tb][:, es], G[:, 32 + hd:33 + hd], ALU.mult, [vTb[tb], G])
                kb.cp(vg, vg[:, 128:129], G[:, 32 + hd:33 + hd], [G], eng="pool")
                pst = banks[2 * ix]
                kb.mm(pst, pst[:, 0:128], qkb[4 + pr_][hs, cs], qkb[pr_][hs, cs], [qkb[4 + pr_], qkb[pr_]])
                yield
                ST = STm[ix]
                kb.tt(ST, ST[:, :], pst[:, 0:128], m_iu, ALU.mult, [pst, cstF])
                pa_ = banks[2 * ix + 1]
                kb.mm(pa_, pa_[:, 0:129], qkb[pr_][hs, cs], C[hs, :], [qkb[pr_], C], start=True, stop=False)
                kb.mm(pa_, pa_[:, 0:129], ST[:, :], vg[:, 0:129], [ST, vg], start=False, stop=True)
                pc_ = banks[2 * ix]
                kb.mm(pc_, pc_[hs, 256:385], kT[:, hh * 64:(hh + 1) * 64], vg[:, 0:129], [kT, vg])
                yield
                ob = osb[ix]
                kb.ts(ob, ob[:, 0:129], pa_[:, 0:129], G[:, 24 + hd:25 + hd], ALU.mult, [pa_, G])
                ct = ctmp[ix]
                kb.tt(ct, ct[hs, 0:129], pc_[hs, 256:385], C[hs, :], ALU.add, [pc_, C])
                kb.ts(C, C[hs, :], ct[hs, 0:129], G[hs, 40 + hd:41 + hd], ALU.mult, [ct, G])
                yield
                kb.act(ob, ob[:, 129:130], ob[:, 128:129], AF.Abs, [ob])
                yield
                kb.ts(ob, ob[:, 129:130], ob[:, 129:130], 1.0, ALU.max, [ob])
                kb.op("dve", lambda e: e.reciprocal(out=ob[:, 130:131], in_=ob[:, 129:130]), reads=[ob], writes=[ob])
                yield
                hb_ = hhb[ix]
                kb.act(hb_, hb_[:, :], ob[:, 0:128], AF.Square, [ob], scale=ob[:, 130:131], accum_out=ob[:, 131:132])
                yield
                kb.act(ob, ob[:, 132:133], ob[:, 131:132], AF.Ln, [ob, cstF], scale=1.0 / 128.0, bias=eps_c)
                yield
                kb.act(ob, ob[:, 133:134], ob[:, 132:133], AF.Exp, [ob], scale=-0.5)
                kb.tt(ob, ob[:, 134:135], ob[:, 133:134], ob[:, 130:131], ALU.mult, [ob])
                kb.stt(hb_, hb_[:, :], ob[:, 0:128], ob[:, 134:135],
                       parT[:, PART_OFF["hnw"] + hd * 128:PART_OFF["hnw"] + (hd + 1) * 128],
                       ALU.mult, ALU.mult, [hb_, ob, parT])
                yield
                kb.tt(hgTb[tb], hgTb[tb][:, es], hb_[:, :], oTb[tb][:, es], ALU.mult, [hb_, oTb[tb]], eng="pool")
                yield

            def fm_gen(tb):
                cs = slice(tb * L, (tb + 1) * L)
                for kc in range(KC):
                    pt = (banks[6], ypsum)[kc % 2]
                    kb.tr(pt, pt[:, 0:128], hgTb[tb][:, kc * 128:(kc + 1) * 128], ident, [hgTb[tb], cstF])
                    kb.cp(ygb[kc], ygb[kc][:, cs], pt[:, 0:128], [pt])
                    yield

            def rr(gens):
                gens = list(gens)
                while gens:
                    for g_ in list(gens):
                        try:
                            next(g_)
                        except StopIteration:
                            gens.remove(g_)

            pend = []
            for tb in range(NCH):
                cs = slice(tb * L, (tb + 1) * L)
                for pr_ in range(4):
                    ptk = (banks[6], ypsum)[pr_ % 2]
                    kb.tr(ptk, ptk[:, 0:128], qkb[4 + pr_][:, cs], ident, [qkb[4 + pr_], cstF])
                    kb.cp(kTm[pr_], kTm[pr_][:, :], ptk[:, 0:128], [ptk])
                for g0 in range(0, 8, NHF):
                    gens = [head_gen(tb, hd) for hd in range(g0, min(8, g0 + NHF))]
                    if g0 == 0 and pend:
                        gens += pend
                        pend = []
                    rr(gens)
                pend = [fm_gen(tb)]
            rr(pend)
            outproj(1, ml_w_out, ygb)

        aoff[0] = 0
        ob_t = aalloc([P, KC, TT])
        obuf = Buf("obuf", ob_t)
        xT_v = xT.rearrange("(k p) t -> p k t", p=P)
        oT_v = outT.rearrange("(k p) t -> p k t", p=P)
        assert n_tiles % 2 == 0
        for tp in range(n_tiles // 2):
            for s_i in range(2):
                XO[0] = s_i * TT
                t0 = (2 * tp + s_i) * TT
                kb.dma(x_t[:, :, XO[0]:XO[0] + TT], xT_v[:, :, t0:t0 + TT], writes=xb)
                if "l0m" in stages:
                    l0_mixer(2 * tp + s_i)
            if "l0f" in stages:
                ffn(0)
            if "l1m" in stages:
                for s_i in range(2):
                    XO[0] = s_i * TT
                    l1_mixer(2 * tp + s_i)
            if "l1f" in stages:
                ffn(1)
            for s_i in range(2):
                XO[0] = s_i * TT
                t0 = (2 * tp + s_i) * TT
                if "fin" in stages:
                    kb.barrier()
                    ssb = pbank()
                    for c in range(KC):
                        s_ = sq[c % 2]
                        kb.act(s_, s_[:, :], xb[c][:, XO[0]:XO[0] + TT], AF.Square, [xb[c]])
                        kb.mm(ssb, ssb[:, 0:TT], onesD, s_[:, :], [s_, cstR], start=(c == 0), stop=(c == KC - 1))
                    kb.act(rs, rs[:, :], ssb[:, 0:TT], AF.Ln, [ssb, cstF], bias=eps_c)
                    kb.act(rs, rs[:, :], rs[:, :], AF.Exp, [rs], scale=-0.5)
                    for c in range(KC):
                        kb.stt(obuf, ob_t[:, c, :], xb[c][:, XO[0]:XO[0] + TT], pc("fw", c), rs[:, :],
                               ALU.mult, ALU.mult, [xb[c], rs, par])
                    kb.dma(oT_v[:, :, t0:t0 + TT], ob_t, reads=[obuf])
                else:
                    kb.dma(oT_v[:, :, t0:t0 + TT], x_t[:, :, XO[0]:XO[0] + TT], reads=xb)
        kb.finish()
        print("instructions", kb.ninst, "waits", kb.nwait, "arena", aoff[1], aoffR[0])
    return nc


TT_DEFAULT = 256


def make_in_maps(inp, TT, batches):
    cstF, cstR = make_consts(TT)
    parT = pack_part(inp)
    f = lambda a: np.ascontiguousarray(np.asarray(a, np.float32))
    shared = {
        "parT": parT, "cstF": cstF, "cstR": cstR,
        "ada_w": f(inp["ada_w"]), "rw_w_in": f(inp["rw_w_in"][0]), "rw_w2": f(inp["rw_w2"][0]),
        "rw_a2": f(inp["rw_a2"][0]), "rw_g2": f(inp["rw_g2"][0]), "rw_w_out": f(inp["rw_w_out"][0]),
        "ml_w_in": f(inp["ml_w_in"][0]), "ml_w_out": f(inp["ml_w_out"][0]),
        "ffn_w_gu": f(inp["ffn_w_gu"]), "ffn_w_down": f(inp["ffn_w_down"]),
    }
    maps = []
    for b in batches:
        m = dict(shared)
        m["xT"] = np.ascontiguousarray(np.asarray(inp["x"][b], np.float32).T)
        m["par"] = pack_par(inp, b)
        maps.append(m)
    return maps


def kernel(**inputs):
    TT = TT_DEFAULT
    nc = build_nc(TT=TT)
    batches = [0, 1, 2, 3, 0, 1, 2, 3]
    in_maps = make_in_maps(inputs, TT, batches)
    res = run_bass_kernel_spmd(nc, in_maps, core_ids=list(range(8)))
    out = np.stack([np.ascontiguousarray(res.results[b]["outT"].T) for b in range(4)], axis=0)
    return out.astype(np.float32)
```
